# Optimizing a Trainium2 kernel written in Bass

```python
import math
import jax
import jax.numpy as jnp
from jax import lax
import numpy as np

D_MODEL = 2048
BATCH = 8
SEQ = 4096
DEPTH = 2

N_META = 16
N_BRANCH = 3
EPS = 1e-6
ROPE_THETA = 10000.0

DA_HEADS = 4
DA_V_DIM = D_MODEL // 2 // DA_HEADS
DA_HEAD_DIM = DA_V_DIM // 2
DA_WIDTH = DA_HEADS * DA_V_DIM
DA_QK_WIDTH = DA_HEADS * 2 * DA_HEAD_DIM
Q_BLOCK = 128

CONV_CH = D_MODEL // 2
CONV_WIDTH = 31

GLA_HEADS = 4
GLA_VW = D_MODEL // 2
GLA_KW = D_MODEL // 4
GLA_DV = GLA_VW // GLA_HEADS
GLA_DK = GLA_KW // GLA_HEADS
GLA_GATE_RANK = 16
GLA_TAU = 16.0
GLA_CHUNK = 64

D_FF = 4 * D_MODEL

IN_SIZES = (DA_QK_WIDTH, DA_QK_WIDTH, DA_WIDTH, 2 * CONV_CH, GLA_KW, GLA_KW, GLA_VW, 2 * GLA_GATE_RANK, GLA_VW, N_BRANCH * D_MODEL)
IN_WIDTH = sum(IN_SIZES)

kernel_name = 'hybrid_gated_diffattn_conformer_gla_encoder'


def rmsnorm(x, g):
    xf = x.astype(jnp.float32)
    y = xf * lax.rsqrt(jnp.mean(xf * xf, axis=-1, keepdims=True) + EPS)
    return (y * g.astype(jnp.float32)).astype(x.dtype)


def layernorm(x, g, b):
    xf = x.astype(jnp.float32)
    mu = jnp.mean(xf, axis=-1, keepdims=True)
    var = jnp.mean(jnp.square(xf - mu), axis=-1, keepdims=True)
    y = (xf - mu) * lax.rsqrt(var + EPS)
    return (y * g.astype(jnp.float32) + b.astype(jnp.float32)).astype(x.dtype)


def rope_tables(T, dim):
    inv_freq = 1.0 / (ROPE_THETA ** (jnp.arange(0, dim, 2, dtype=jnp.float32) / dim))
    ang = jnp.arange(T, dtype=jnp.float32)[:, None] * inv_freq[None, :]
    return jnp.cos(ang), jnp.sin(ang)


def apply_rope(x, cos, sin):
    half = x.shape[-1] // 2
    x1 = x[..., :half].astype(jnp.float32)
    x2 = x[..., half:].astype(jnp.float32)
    return jnp.concatenate([x1 * cos - x2 * sin, x2 * cos + x1 * sin], axis=-1).astype(x.dtype)


def diff_attention(q, k, v, lam, lam_init, subln_g):
    B, T, _ = q.shape
    H, d = DA_HEADS, DA_HEAD_DIM
    q = q.reshape(B, T, H, 2, d).transpose(0, 2, 3, 1, 4)
    k = k.reshape(B, T, H, 2, d).transpose(0, 2, 3, 1, 4)
    v = v.reshape(B, T, H, DA_V_DIM).transpose(0, 2, 1, 3)
    cos, sin = rope_tables(T, d)
    q = apply_rope(q, cos, sin) * (d ** -0.5)
    k = apply_rope(k, cos, sin)
    n_blk = -(-T // Q_BLOCK)
    pad = n_blk * Q_BLOCK - T
    qp = jnp.pad(q, ((0, 0), (0, 0), (0, 0), (0, pad), (0, 0)))
    qb = qp.reshape(B, H, 2, n_blk, Q_BLOCK, d).transpose(3, 0, 1, 2, 4, 5)

    def block(q_blk):
        s = jnp.einsum('bhmqd,bhmkd->bhmqk', q_blk, k).astype(jnp.float32)
        p = jax.nn.softmax(s, axis=-1)
        a = p[:, :, 0] - lam * p[:, :, 1]
        return jnp.einsum('bhqk,bhkv->bhqv', a.astype(v.dtype), v)

    o = lax.map(block, qb)
    o = o.transpose(1, 2, 0, 3, 4).reshape(B, H, n_blk * Q_BLOCK, DA_V_DIM)[:, :, :T]
    o = rmsnorm(o, subln_g) * (1.0 - lam_init)
    return o.transpose(0, 2, 1, 3).reshape(B, T, DA_WIDTH)


def conv_module(u, dw_w, dw_b, ln_g, ln_b):
    a, gate = jnp.split(u, 2, axis=-1)
    z = a * jax.nn.sigmoid(gate)
    half = CONV_WIDTH // 2
    z = lax.conv_general_dilated(z, dw_w.astype(z.dtype)[:, None, :], window_strides=(1,), padding=[(half, half)], dimension_numbers=('NWC', 'WIO', 'NWC'), feature_group_count=CONV_CH) + dw_b.astype(z.dtype)
    z = layernorm(z, ln_g, ln_b)
    return jax.nn.silu(z)


def gla_scan(q, k, v, log_a):
    B, H, L, dk = q.shape
    dv = v.shape[-1]
    n = L // GLA_CHUNK

    def to_chunks(t):
        return t.reshape(B, H, n, GLA_CHUNK, t.shape[-1]).transpose(2, 0, 1, 3, 4)

    mask = jnp.tril(jnp.ones((GLA_CHUNK, GLA_CHUNK), dtype=bool))

    def step(S, inp):
        qc, kc, vc, gc = inp
        b = jnp.cumsum(gc, axis=2)
        o_inter = jnp.einsum('bhcd,bhdv->bhcv', qc * jnp.exp(b), S)
        rel = jnp.where(mask[:, :, None], b[:, :, :, None, :] - b[:, :, None, :, :], -jnp.inf)
        A = jnp.einsum('bhid,bhjd,bhijd->bhij', qc, kc, jnp.exp(rel))
        o_intra = jnp.einsum('bhij,bhjv->bhiv', A, vc)
        b_last = b[:, :, -1, :]
        S = S * jnp.exp(b_last)[..., None] + jnp.einsum('bhcd,bhcv->bhdv', kc * jnp.exp(b_last[:, :, None, :] - b), vc)
        return S, o_inter + o_intra

    S0 = jnp.zeros((B, H, dk, dv), jnp.float32)
    _, o = lax.scan(step, S0, (to_chunks(q), to_chunks(k), to_chunks(v), to_chunks(log_a)))
    return o.transpose(1, 2, 0, 3, 4).reshape(B, H, L, dv)


def gla_bidirectional(q, k, v, gate_lr, r, gw_f, gb_f, gw_b, gb_b, norm_g):
    B, T, _ = q.shape
    f32 = jnp.float32
    lr_f, lr_b = jnp.split(gate_lr.astype(f32), 2, axis=-1)
    la_f = jax.nn.log_sigmoid(lr_f @ gw_f.astype(f32) + gb_f.astype(f32)) / GLA_TAU
    la_b = jax.nn.log_sigmoid(lr_b @ gw_b.astype(f32) + gb_b.astype(f32)) / GLA_TAU

    def heads(t, d):
        return t.astype(f32).reshape(B, T, GLA_HEADS, d).transpose(0, 2, 1, 3)

    lead = (GLA_CHUNK - N_META % GLA_CHUNK) % GLA_CHUNK

    def pad(t):
        return jnp.pad(t, ((0, 0), (0, 0), (lead, 0), (0, 0)))

    qh = pad(heads(q, GLA_DK) * (GLA_DK ** -0.5))
    kh = pad(heads(k, GLA_DK))
    vh = pad(heads(v, GLA_DV))
    af = pad(heads(la_f, GLA_DK))
    ab = pad(heads(la_b, GLA_DK))
    o_f = gla_scan(qh, kh, vh, af)
    o_b = jnp.flip(gla_scan(jnp.flip(qh, 2), jnp.flip(kh, 2), jnp.flip(vh, 2), jnp.flip(ab, 2)), 2)
    o = (o_f + o_b)[:, :, lead:]
    o = rmsnorm(o, norm_g).transpose(0, 2, 1, 3).reshape(B, T, GLA_VW)
    return o.astype(r.dtype) * jax.nn.silu(r)


def setup_inputs(seed: int = 0) -> dict:
    key = jax.random.key(seed)
    ks = jax.random.split(key, 24)
    f32 = jnp.float32
    L, D = DEPTH, D_MODEL

    def nrm(k, shape, scale):
        return jax.random.normal(k, shape, f32) * scale

    return {
        'x': nrm(ks[0], (BATCH, SEQ, D), 1.0),
        'meta_tokens': nrm(ks[1], (N_META, D), 1.0),
        'mix_norm_g': 1.0 + nrm(ks[2], (L, D), 0.02),
        'w_in': nrm(ks[3], (L, D, IN_WIDTH), D ** -0.5),
        'da_lambda': nrm(ks[4], (L, 4, DA_HEAD_DIM), 0.1),
        'da_subln_g': 1.0 + nrm(ks[5], (L, DA_V_DIM), 0.02),
        'w_da_proj': nrm(ks[6], (L, DA_WIDTH, D), DA_WIDTH ** -0.5),
        'conv_dw_w': nrm(ks[7], (L, CONV_WIDTH, CONV_CH), CONV_WIDTH ** -0.5),
        'conv_dw_b': nrm(ks[8], (L, CONV_CH), 0.02),
        'conv_ln_g': 1.0 + nrm(ks[9], (L, CONV_CH), 0.02),
        'conv_ln_b': nrm(ks[10], (L, CONV_CH), 0.02),
        'w_conv_proj': nrm(ks[11], (L, CONV_CH, D), CONV_CH ** -0.5),
        'b_conv_proj': nrm(ks[12], (L, D), 0.02),
        'gla_gate_w_fwd': nrm(ks[13], (L, GLA_GATE_RANK, GLA_KW), GLA_GATE_RANK ** -0.5),
        'gla_gate_b_fwd': nrm(ks[14], (L, GLA_KW), 0.1),
        'gla_gate_w_bwd': nrm(ks[15], (L, GLA_GATE_RANK, GLA_KW), GLA_GATE_RANK ** -0.5),
        'gla_gate_b_bwd': nrm(ks[16], (L, GLA_KW), 0.1),
        'gla_norm_g': 1.0 + nrm(ks[17], (L, GLA_DV), 0.02),
        'w_gla_proj': nrm(ks[18], (L, GLA_VW, D), GLA_VW ** -0.5),
        'w_out': nrm(ks[19], (L, D, D), D ** -0.5),
        'mlp_norm_g': 1.0 + nrm(ks[20], (L, D), 0.02),
        'w_mlp_in': nrm(ks[21], (L, D, D_FF), D ** -0.5),
        'w_mlp_out': nrm(ks[22], (L, D_FF, D), D_FF ** -0.5),
        'final_norm_g': 1.0 + nrm(ks[23], (D,), 0.02),
    }


def reference(x, meta_tokens, mix_norm_g, w_in, da_lambda, da_subln_g, w_da_proj, conv_dw_w, conv_dw_b, conv_ln_g, conv_ln_b, w_conv_proj, b_conv_proj, gla_gate_w_fwd, gla_gate_b_fwd, gla_gate_w_bwd, gla_gate_b_bwd, gla_norm_g, w_gla_proj, w_out, mlp_norm_g, w_mlp_in, w_mlp_out, final_norm_g):
    B = x.shape[0]
    h = jnp.concatenate([jnp.broadcast_to(meta_tokens[None].astype(x.dtype), (B, N_META, D_MODEL)), x], axis=1)
    T = h.shape[1]
    split_at = [int(i) for i in np.cumsum(IN_SIZES)[:-1]]
    for l in range(DEPTH):
        lam_init = 0.8 - 0.6 * math.exp(-0.3 * l)
        a = rmsnorm(h, mix_norm_g[l])
        u = a @ w_in[l]
        dq, dk, dv, cu, gq, gk, gv, glr, gr, gates = jnp.split(u, split_at, axis=-1)
        lq1, lk1, lq2, lk2 = da_lambda[l].astype(jnp.float32)
        lam = jnp.exp(jnp.sum(lq1 * lk1)) - jnp.exp(jnp.sum(lq2 * lk2)) + lam_init
        y_da = diff_attention(dq, dk, dv, lam, lam_init, da_subln_g[l]) @ w_da_proj[l]
        y_cv = conv_module(cu, conv_dw_w[l], conv_dw_b[l], conv_ln_g[l], conv_ln_b[l]) @ w_conv_proj[l] + b_conv_proj[l]
        y_gla = gla_bidirectional(gq, gk, gv, glr, gr, gla_gate_w_fwd[l], gla_gate_b_fwd[l], gla_gate_w_bwd[l], gla_gate_b_bwd[l], gla_norm_g[l]) @ w_gla_proj[l]
        g = jax.nn.sigmoid(gates.reshape(B, T, N_BRANCH, D_MODEL))
        merged = g[:, :, 0] * y_da + g[:, :, 1] * y_cv + g[:, :, 2] * y_gla
        h = h + merged @ w_out[l]
        a = rmsnorm(h, mlp_norm_g[l])
        h = h + jnp.square(jax.nn.relu(a @ w_mlp_in[l])) @ w_mlp_out[l]
    return rmsnorm(h, final_norm_g)[:, N_META:]
```

```python
import math
from contextlib import ExitStack
import numpy as np
import ml_dtypes
import concourse.bass as bass
import concourse.mybir as mybir
from concourse.bass_utils import run_bass_kernel_spmd

F32 = mybir.dt.float32
BF16 = mybir.dt.bfloat16
AF = mybir.ActivationFunctionType
ALU = mybir.AluOpType

D = 2048
SEQ = 4096
NMETA = 16
T = SEQ + NMETA
DEPTH = 2
KC = D // 128
EPS = 1e-6
DFF = 4 * D
IN_SIZES = (1024, 1024, 1024, 2048, 512, 512, 1024, 32, 1024, 6144)
IN_W = sum(IN_SIZES)
BLOCKS = [(0, NMETA)] + [(NMETA + 512 * i, 512) for i in range(8)]
TILES = [(0, NMETA)] + [(NMETA + 128 * i, 128) for i in range(32)]


def block_tiles(bi):
    t0, nt = BLOCKS[bi]
    if nt <= 128:
        return [(t0, nt)]
    return [(t0 + 128 * i, 128) for i in range(nt // 128)]


_UNC = [0]


def UN(name):
    _UNC[0] += 1
    return f"{name}_{_UNC[0]}"


class Buf:
    __slots__ = ("name", "wr", "rd", "sem", "semname", "dcnt")

    def __init__(self, name):
        self.name = name
        self.wr = {}
        self.rd = {}
        self.sem = None
        self.semname = None
        self.dcnt = 0


class FW:
    def __init__(self, nc, es):
        self.nc = nc
        self.es = es
        self.eng = {"pe": nc.tensor, "act": nc.scalar, "dve": nc.vector, "pool": nc.gpsimd, "sp": nc.sync}
        self.sems = {}
        self.latest = {}
        self.ecnt = {}
        for k in self.eng:
            nm = "e_" + k
            self.sems[nm] = es.enter_context(nc.semaphore(nm))
            self.latest[nm] = 0
            self.ecnt[k] = 0
        self.waited = {k: {} for k in self.eng}
        self.nbuf = 0
        self.ninst = 0
        self.free = []
        self.owners = []
        self.semval = {}

    def buf(self, name="b"):
        self.nbuf += 1
        return Buf(f"{name}{self.nbuf}")

    def _need(self, reads, writes, pwrites):
        need = {}
        for b in reads:
            for k, v in b.wr.items():
                if need.get(k, 0) < v:
                    need[k] = v
        for b in writes:
            for dct in (b.wr, b.rd):
                for k, v in dct.items():
                    if need.get(k, 0) < v:
                        need[k] = v
        for b in pwrites:
            for k, v in b.rd.items():
                if need.get(k, 0) < v:
                    need[k] = v
        return need

    def _waits(self, e, need):
        w = self.waited[e]
        own = "e_" + e
        for k, v in need.items():
            if w.get(k, 0) >= v:
                continue
            if k == own:
                if e == "pe" or v < self.ecnt[e] - 1:
                    continue
            self.eng[e].wait_ge(self.sems[k], v)
            w[k] = v

    def _record(self, key, val, reads, writes, pwrites):
        self.latest[key] = val
        for b in reads:
            if b.rd.get(key, 0) < val:
                b.rd[key] = val
        for b in writes:
            b.wr = {key: val}
            b.rd = {}
        for b in pwrites:
            if b.wr.get(key, 0) < val:
                b.wr[key] = val

    def op(self, e, fn, reads=(), writes=(), pwrites=()):
        self._waits(e, self._need(reads, writes, pwrites))
        inst = fn(self.eng[e])
        self.ecnt[e] += 1
        key = "e_" + e
        inst.then_inc(self.sems[key], 1)
        self._record(key, self.ecnt[e], reads, writes, pwrites)
        self.ninst += 1
        return inst

    def dma(self, q, out, in_, reads=(), writes=(), owner=None, **kw):
        self._waits(q, self._need(reads, writes, ()))
        if owner is None:
            owner = writes[0] if writes else reads[0]
        if owner.sem is None:
            if self.free:
                owner.semname = self.free.pop()
            else:
                owner.semname = "d_" + owner.name
                self.sems[owner.semname] = self.es.enter_context(self.nc.semaphore(owner.semname))
                self.semval[owner.semname] = 0
            owner.sem = self.sems[owner.semname]
            self.owners.append(owner)
        inst = self.eng[q].dma_start(out=out, in_=in_, **kw)
        owner.dcnt = self.semval[owner.semname] + 16
        self.semval[owner.semname] = owner.dcnt
        inst.then_inc(owner.sem, 16)
        self._record(owner.semname, owner.dcnt, reads, writes, ())
        self.ninst += 1
        return inst

    def barrier(self, engines=None):
        for e in (engines or self.eng):
            self._waits_all(e)
        for b in self.owners:
            self.free.append(b.semname)
            b.sem = None
        self.owners = []

    def _waits_all(self, e):
        w = self.waited[e]
        for k, v in self.latest.items():
            if v > w.get(k, 0) and k != "e_" + e:
                self.eng[e].wait_ge(self.sems[k], v)
                w[k] = v


class Ring:
    def __init__(self, fw, es, name, shape, dtype, n):
        self.t = es.enter_context(fw.nc.sbuf_tensor(UN(name), [128, n] + list(shape), dtype))
        self.bufs = [fw.buf(name) for _ in range(n)]
        self.n = n
        self.i = 0

    def next(self):
        j = self.i % self.n
        self.i += 1
        return self.t[:, j], self.bufs[j]


class PsumPool:
    def __init__(self, fw, es, n=8):
        self.ts = [es.enter_context(fw.nc.psum_tensor(f"ps{i}", [128, 512], F32)) for i in range(n)]
        self.bufs = [fw.buf("ps") for _ in range(n)]
        self.n = n
        self.i = 0
        self.rot = list(range(n))

    def next(self):
        j = self.rot[self.i % len(self.rot)]
        self.i += 1
        return self.ts[j], self.bufs[j]

    def bank(self, j):
        return self.ts[j], self.bufs[j]


def w_in_groups():
    gs = []
    col = 0
    for s, w in enumerate(IN_SIZES):
        gc = 512 if w >= 512 else w
        for j in range(w // gc):
            gs.append((s, col + j * gc, gc, j))
        col += w
    return gs


W_SPECS = {
    "w_in": (D, IN_W, None),
    "w_da_proj": (1024, D, 512),
    "w_conv_proj": (1024, D, 512),
    "w_gla_proj": (1024, D, 512),
    "w_out": (D, D, 512),
    "w_mlp_in": (D, DFF, 512),
    "w_mlp_out": (DFF, D, 128),
}


def w_groups(name):
    K, M, gc = W_SPECS[name]
    if name == "w_in":
        return [(c0, g) for (_, c0, g, _) in w_in_groups()]
    return [(c0, gc) for c0 in range(0, M, gc)]


def w_scratch_elems(name):
    K, M, _ = W_SPECS[name]
    return K * M


class Prog:
    def __init__(self, cfg):
        self.cfg = cfg
        self.nc = bass.Bass("TRN2", target_bir_lowering=False)
        self.dbg_outs = []
        self.in_names = []

    def din(self, name, shape, dtype=F32):
        self.in_names.append(name)
        return self.nc.dram_tensor(name, list(shape), dtype, kind="ExternalInput").ap()

    def dscr(self, name, shape, dtype):
        kind = "ExternalOutput" if name in self.cfg.get("dump", ()) else "Internal"
        if kind == "ExternalOutput":
            self.dbg_outs.append(name)
        return self.nc.dram_tensor(name, list(shape), dtype, kind=kind).ap()


def flat_w_offsets(name):
    K, M, _ = W_SPECS[name]
    offs = []
    o = 0
    for (c0, gc) in w_groups(name):
        offs.append(o)
        o += K * gc
    return offs


def build_program(cfg):
    P = Prog(cfg)
    nc = P.nc
    phases = cfg.get("phases", None)
    layers = cfg.get("layers", list(range(DEPTH)))

    def on(ph):
        return phases is None or ph in phases

    xT = P.din("xT", [KC, 128, SEQ])
    metaT = P.din("metaT", [KC, 128, NMETA])
    Wd = {n: P.din(n, [DEPTH, W_SPECS[n][0], W_SPECS[n][1]]) for n in cfg.get("wcast", list(W_SPECS))}
    mixg = P.din("mixg", [DEPTH, 128, KC])
    mlpg = P.din("mlpg", [DEPTH, 128, KC])
    fing = P.din("fing", [128, KC])
    lamT = P.din("lamT", [DEPTH, 128, 4])
    sublng = P.din("sublng", [DEPTH, 128, 2])
    convw = P.din("convw", [DEPTH, 128, 8, 31])
    convb = P.din("convb", [DEPTH, 128, 8])
    convlg = P.din("convlg", [DEPTH, 128, 8])
    convlb = P.din("convlb", [DEPTH, 128, 8])
    bconv = P.din("bconv", [DEPTH, 128, KC])
    gwb = P.din("gwb", [DEPTH, 2, 17, 512])
    glang = P.din("glang", [DEPTH, 128, 2])
    cosT = P.din("cosT", [128, T])
    sinT = P.din("sinT", [128, T])
    perm_d = P.din("perm", [128, 128], BF16)
    ident_d = P.din("ident", [128, 128])
    gmat_d = P.din("gmat", [128, 6, 128], BF16)
    cind_d = P.din("cind", [128, 2], BF16)

    outT = nc.dram_tensor("outT", [KC, 128, SEQ], F32, kind="ExternalOutput").ap()

    wscr = {}
    for n in W_SPECS:
        for l in range(DEPTH):
            wscr[(n, l)] = P.dscr(f"wb_{n}_{l}", [w_scratch_elems(n)], BF16)
    hT = P.dscr("hT", [KC, 128, T], F32)
    qT = P.dscr("qT", [8, 128, T], BF16)
    kT = P.dscr("kT", [8, 128, T], BF16)
    vda = P.dscr("vda", [T, 1024], BF16)
    cuA = P.dscr("cuA", [8, 128, T], BF16)
    cuG = P.dscr("cuG", [8, 128, T], BF16)
    gqT = P.dscr("gqT", [4, 128, T], BF16)
    gkT = P.dscr("gkT", [4, 128, T], BF16)
    gktok = P.dscr("gktok", [T, 512], BF16)
    gvtok = P.dscr("gvtok", [T, 1024], BF16)
    glrT = P.dscr("glrT", [32, T], F32)
    srT = P.dscr("srT", [8, 128, T], BF16)
    gatesT = P.dscr("gatesT", [48, 128, T], BF16)
    odaT = P.dscr("odaT", [8, 128, T], BF16)
    ocvT = P.dscr("ocvT", [8, 128, T], BF16)
    oglaT = P.dscr("oglaT", [8, 128, T], BF16)

    es0 = ExitStack()
    with es0:
        fw = FW(nc, es0)
        ps = PsumPool(fw, es0)
        ones_bf = es0.enter_context(nc.sbuf_tensor(UN("ones_bf"), [128, 128], BF16))
        ones_f = es0.enter_context(nc.sbuf_tensor(UN("ones_f"), [128, 128], F32))
        b_ones = fw.buf("ones")
        fw.op("pool", lambda e: e.memset(ones_bf[:], 1.0), writes=[b_ones])
        fw.op("pool", lambda e: e.memset(ones_f[:], 1.0), writes=[b_ones])
        b_ones.rd = {}

        def h_src(l, bi):
            t0, nt = BLOCKS[bi]
            if l == 0:
                if bi == 0:
                    return metaT.rearrange("c p t -> p c t")
                return xT[:, :, t0 - NMETA:t0 - NMETA + nt].rearrange("c p t -> p c t")
            return hT[:, :, t0:t0 + nt].rearrange("c p t -> p c t")

        def wgroup_ap(name, l, gi):
            K = W_SPECS[name][0]
            c0, gc = w_groups(name)[gi]
            off = flat_w_offsets(name)[gi]
            return wscr[(name, l)][off:off + K * gc].rearrange("(p r) -> p r", p=128)

        def phase_wcast(l, names):
            with ExitStack() as es:
                st = Ring(fw, es, "wc_st", [4096], F32, 3)
                bo = Ring(fw, es, "wc_bf", [4096], BF16, 3)
                k = 0
                for name in names:
                    K, M, _ = W_SPECS[name]
                    KCn = K // 128
                    for gi, (c0, gc) in enumerate(w_groups(name)):
                        kcs = min(KCn, 4096 // gc)
                        gap = wgroup_ap(name, l, gi)
                        for kc0 in range(0, KCn, kcs):
                            sa, sb = st.next()
                            ba, bb = bo.next()
                            src = Wd[name][l, kc0 * 128:(kc0 + kcs) * 128, c0:c0 + gc].rearrange("(kc p) c -> p kc c", p=128)
                            n = kcs * gc
                            fw.dma("sp", sa[:, :n].rearrange("p (kc c) -> p kc c", c=gc), src, writes=[sb])
                            if k % 2 == 0:
                                fw.op("act", lambda e: e.copy(out=ba[:, :n], in_=sa[:, :n]), reads=[sb], writes=[bb])
                            else:
                                fw.op("dve", lambda e: e.tensor_copy(out=ba[:, :n], in_=sa[:, :n]), reads=[sb], writes=[bb])
                            fw.dma("pool", gap[:, kc0 * gc:kc0 * gc + n], ba[:, :n], reads=[bb], owner=bb)
                            k += 1
                fw.barrier()

        def rmsnorm_T(hbuf_ap, hb, nt, g_ap, out_fn, rings):
            sqr, rstd_r = rings
            pt, pb = ps.next()
            for c in range(KC):
                sa, sb = sqr.next()
                fw.op("act", lambda e: e.activation(out=sa[:, :nt], in_=hbuf_ap[:, c, :nt], func=AF.Square), reads=[hb], writes=[sb])
                fw.op("pe", lambda e: e.matmul(pt[:, :nt], lhsT=ones_bf[:], rhs=sa[:, :nt], start=(c == 0), stop=(c == KC - 1)), reads=[sb], writes=[pb])
            ra, rb = rstd_r.next()
            fw.op("act", lambda e: e.activation(out=ra[:, :nt], in_=pt[:, :nt], func=AF.Sqrt, bias=EPS, scale=1.0 / D), reads=[pb], writes=[rb])
            fw.op("dve", lambda e: e.reciprocal(out=ra[:, :nt], in_=ra[:, :nt]), reads=[rb], writes=[rb])
            return ra, rb

        P.fw = fw
        P.ps = ps
        import types
        C = types.SimpleNamespace(**locals())
        build_phases(C)
        fw.barrier()
    return P


def build_phases(C):
    P, fw, ps, nc = C.P, C.fw, C.ps, C.nc
    cfg = P.cfg
    phases = cfg.get("phases", None)
    layers = cfg.get("layers", list(range(DEPTH)))

    def on(ph):
        return phases is None or ph in phases

    for l in layers:
        names = cfg.get("wcast", list(W_SPECS))
        if on("wcast"):
            C.phase_wcast(l, names)
    for l in layers:
        if on("A"):
            phase_A(C, l)
        if on("DA"):
            phase_DA(C, l)
        if on("CV"):
            phase_CV(C, l)
        if on("GLA"):
            phase_GLA(C, l)
        if on("MRG"):
            phase_MRG(C, l)
        if on("MLP"):
            phase_MLP(C, l)
    if on("FIN"):
        phase_FIN(C)


def load_small(C, es, name, src_ap, shape, dtype=F32, q="sp"):
    t = es.enter_context(C.nc.sbuf_tensor(UN(name), list(shape), dtype))
    b = C.fw.buf(name)
    C.fw.dma(q, t[:], src_ap, writes=[b])
    return t, b


def phase_A(C, l):
    P, fw, ps, nc = C.P, C.fw, C.ps, C.nc
    groups = w_in_groups()
    with ExitStack() as es:
        g_t, g_b = load_small(C, es, "A_g", C.mixg[l], [128, KC])
        perm_t, perm_b = load_small(C, es, "A_perm", C.perm_d, [128, 128], BF16)
        hbuf = es.enter_context(nc.sbuf_tensor(UN("A_h"), [128, KC, 512], F32))
        hb = fw.buf("A_h")
        sqr = Ring(fw, es, "A_sq", [512], BF16, 3)
        rstd_r = Ring(fw, es, "A_rstd", [512], F32, 2)
        aT_r = Ring(fw, es, "A_aT", [KC, 512], BF16, 2)
        w_r = Ring(fw, es, "A_w", [8192], BF16, 2)
        stg_r = Ring(fw, es, "A_stg", [4, 512], BF16, 3)
        stgf_r = Ring(fw, es, "A_stgf", [512], F32, 2)
        cs_r = Ring(fw, es, "A_cs", [2, 512], F32, 2)
        xb_r = Ring(fw, es, "A_xb", [512], BF16, 2)
        t1_r = Ring(fw, es, "A_t1", [512], F32, 2)
        t2_r = Ring(fw, es, "A_t2", [512], F32, 2)
        nev = 0
        for bi, (t0, nt) in enumerate(BLOCKS):
            if bi not in P.cfg.get("A_blocks", range(9)):
                continue
            fw.dma("sp", hbuf[:, :, :nt], C.h_src(l, bi), writes=[hb])
            ra, rb = C.rmsnorm_T(hbuf, hb, nt, g_t, None, (sqr, rstd_r))
            aT, ab = aT_r.next()
            for c in range(KC):
                fw.op("dve", lambda e: e.scalar_tensor_tensor(out=aT[:, c, :nt], in0=hbuf[:, c, :nt], scalar=g_t[:, c:c + 1], in1=ra[:, :nt], op0=ALU.mult, op1=ALU.mult),
                      reads=[hb, g_b, rb], pwrites=[ab] if c else (), writes=() if c else [ab])
            cs, csb = cs_r.next()
            fw.dma("sp", cs[:, 0, :nt], C.cosT[:, t0:t0 + nt], writes=[csb])
            fw.dma("sp", cs[:, 1, :nt], C.sinT[:, t0:t0 + nt], reads=(), writes=(), owner=csb)
            csb.wr = {csb.semname: csb.dcnt}
            tiles = block_tiles(bi)
            for gi, (sec, c0, gc, j) in enumerate(groups):
                if sec not in P.cfg.get("A_secs", range(10)):
                    continue
                wa, wb = w_r.next()
                fw.dma("sp", wa[:, :KC * gc], C.wgroup_ap("w_in", l, gi), writes=[wb])
                wv = wa[:, :KC * gc].rearrange("p (k c) -> p k c", c=gc)
                nch = max(1, gc // 128)
                if sec in (2, 5, 6):
                    dst = {2: C.vda, 5: C.gktok, 6: C.gvtok}[sec]
                    dcol = j * gc
                    sa, sb = stg_r.next()
                    for ti, (tt0, tn) in enumerate(tiles):
                        pt, pb = ps.next()
                        for k in range(KC):
                            fw.op("pe", lambda e: e.matmul(pt[:tn, :gc], lhsT=aT[:, k, tt0 - t0:tt0 - t0 + tn], rhs=wv[:, k, :], start=(k == 0), stop=(k == KC - 1)),
                                  reads=[ab, wb], writes=[pb])
                        eng = "act" if nev % 2 == 0 else "dve"
                        nev += 1
                        if eng == "act":
                            fw.op("act", lambda e: e.copy(out=sa[:tn, ti, :gc], in_=pt[:tn, :gc]), reads=[pb], pwrites=[sb] if ti else (), writes=() if ti else [sb])
                        else:
                            fw.op("dve", lambda e: e.tensor_copy(out=sa[:tn, ti, :gc], in_=pt[:tn, :gc]), reads=[pb], pwrites=[sb] if ti else (), writes=() if ti else [sb])
                    ntl = len(tiles)
                    tn = tiles[0][1]
                    dap = dst[t0:t0 + nt, dcol:dcol + gc].rearrange("(n p) c -> p n c", p=tn)
                    fw.dma("pool", dap, sa[:tn, :ntl, :gc], reads=[sb], owner=sb)
                    if sec != 5:
                        continue
                if sec == 7:
                    pt, pb = ps.next()
                    for k in range(KC):
                        fw.op("pe", lambda e: e.matmul(pt[:32, :nt], lhsT=wv[:, k, :], rhs=aT[:, k, :nt], start=(k == 0), stop=(k == KC - 1)), reads=[ab, wb], writes=[pb])
                    sa, sb = stgf_r.next()
                    fw.op("dve", lambda e: e.tensor_copy(out=sa[:32, :nt], in_=pt[:32, :nt]), reads=[pb], writes=[sb])
                    fw.dma("pool", C.glrT[:, t0:t0 + nt], sa[:32, :nt], reads=[sb], owner=sb)
                    continue
                dst, func = {0: (C.qT, None), 1: (C.kT, None), 3: (C.cuA if j < 2 else C.cuG, None if j < 2 else AF.Sigmoid),
                             4: (C.gqT, None), 5: (C.gkT, None), 8: (C.srT, AF.Silu), 9: (C.gatesT, AF.Sigmoid)}[sec]
                jj = (j - 2) if (sec == 3 and j >= 2) else j
                sa, sb = stg_r.next()
                for ci in range(nch):
                    pt, pb = ps.next()
                    for k in range(KC):
                        fw.op("pe", lambda e: e.matmul(pt[:, :nt], lhsT=wv[:, k, ci * 128:(ci + 1) * 128], rhs=aT[:, k, :nt], start=(k == 0), stop=(k == KC - 1)), reads=[ab, wb], writes=[pb])
                    wr = dict(pwrites=[sb] if ci else (), writes=() if ci else [sb])
                    if sec in (0, 1):
                        xa, xbb = xb_r.next()
                        fw.op("act", lambda e: e.copy(out=xa[:, :nt], in_=pt[:, :nt]), reads=[pb], writes=[xbb])
                        p2, p2b = ps.next()
                        fw.op("pe", lambda e: e.matmul(p2[:, :nt], lhsT=perm_t[:], rhs=xa[:, :nt], start=True, stop=True), reads=[perm_b, xbb], writes=[p2b])
                        t1, t1b = t1_r.next()
                        t2, t2b = t2_r.next()
                        if True:
                            fw.op("act", lambda e: e.copy(out=t1[:, :nt], in_=pt[:, :nt]), reads=[pb], writes=[t1b])
                            fw.op("act", lambda e: e.copy(out=t2[:, :nt], in_=p2[:, :nt]), reads=[p2b], writes=[t2b])
                            fw.op("dve", lambda e: e.tensor_tensor(out=t1[:, :nt], in0=t1[:, :nt], in1=cs[:, 0, :nt], op=ALU.mult), reads=[t1b, csb], writes=[t1b])
                            fw.op("dve", lambda e: e.tensor_tensor(out=t2[:, :nt], in0=t2[:, :nt], in1=cs[:, 1, :nt], op=ALU.mult), reads=[t2b, csb], writes=[t2b])
                        else:
                            fw.op("dve", lambda e: e.tensor_tensor(out=t1[:, :nt], in0=pt[:, :nt], in1=cs[:, 0, :nt], op=ALU.mult), reads=[pb, csb], writes=[t1b])
                            fw.op("dve", lambda e: e.tensor_tensor(out=t2[:, :nt], in0=p2[:, :nt], in1=cs[:, 1, :nt], op=ALU.mult), reads=[p2b, csb], writes=[t2b])
                        fw.op("dve", lambda e: e.tensor_tensor(out=sa[:, ci, :nt], in0=t1[:, :nt], in1=t2[:, :nt], op=ALU.add), reads=[t1b, t2b], **wr)
                    elif func is not None:
                        fw.op("act", lambda e: e.activation(out=sa[:, ci, :nt], in_=pt[:, :nt], func=func), reads=[pb], **wr)
                    else:
                        eng = "act" if nev % 2 == 0 else "dve"
                        nev += 1
                        if eng == "act":
                            fw.op("act", lambda e: e.copy(out=sa[:, ci, :nt], in_=pt[:, :nt]), reads=[pb], **wr)
                        else:
                            fw.op("dve", lambda e: e.tensor_copy(out=sa[:, ci, :nt], in_=pt[:, :nt]), reads=[pb], **wr)
                ch0 = jj * nch
                dap = dst[ch0:ch0 + nch, :, t0:t0 + nt].rearrange("c p t -> p c t")
                fw.dma("pool", dap, sa[:, :nch, :nt], reads=[sb], owner=sb)
        fw.barrier()


def host_consts():
    bf = ml_dtypes.bfloat16
    d = 128
    inv_freq = (1.0 / (10000.0 ** (np.arange(0, d, 2, dtype=np.float32) / np.float32(d)))).astype(np.float32)
    ang = (np.arange(T, dtype=np.float32)[:, None] * inv_freq[None, :]).astype(np.float32)
    cos = np.cos(ang).astype(np.float32).T
    sin = np.sin(ang).astype(np.float32).T
    cosT = np.concatenate([cos, cos], 0)
    sinT = np.concatenate([-sin, sin], 0)
    perm = np.zeros((128, 128), np.float32)
    for m in range(128):
        perm[(m + 64) % 128, m] = 1.0
    j = np.arange(128)[:, None]
    i = np.arange(128)[None, :]
    same = (j // 64) == (i // 64)
    s = -1.0 / 16.0
    triF = np.where(same & (j <= i), s, 0.0)
    triB = np.where(same & (j >= i), s, 0.0)
    strF = np.where(same & (j > i), s, 0.0)
    strB = np.where(same & (j < i), s, 0.0)
    maskF = np.where(same & (j <= i), 1.0, 0.0)
    maskB = np.where(same & (j >= i), 1.0, 0.0)
    gmat = np.stack([triF, triB, strF, strB, maskF, maskB], 1).astype(bf)
    cind = np.zeros((128, 2), np.float32)
    cind[:64, 0] = s
    cind[64:, 1] = s
    return dict(cosT=np.ascontiguousarray(cosT), sinT=np.ascontiguousarray(sinT), perm=perm.astype(bf),
                ident=np.eye(128, dtype=np.float32), gmat=np.ascontiguousarray(gmat), cind=cind.astype(bf))


def pc(v, nchunk):
    sh = v.shape[:-1]
    return np.ascontiguousarray(np.swapaxes(v.reshape(sh + (nchunk, 128)), -1, -2))


def make_shared_inputs(inp):
    f = np.float32
    sh = {}
    for n in W_SPECS:
        sh[n] = np.ascontiguousarray(inp[n], dtype=f)
    sh["metaT"] = np.ascontiguousarray(inp["meta_tokens"].T.reshape(KC, 128, NMETA), dtype=f)
    sh["mixg"] = pc(inp["mix_norm_g"], KC)
    sh["mlpg"] = pc(inp["mlp_norm_g"], KC)
    sh["fing"] = pc(inp["final_norm_g"], KC)
    sh["lamT"] = np.ascontiguousarray(np.swapaxes(inp["da_lambda"], 1, 2))
    sh["sublng"] = pc(inp["da_subln_g"], 2)
    sh["convw"] = np.ascontiguousarray(inp["conv_dw_w"].reshape(DEPTH, 31, 8, 128).transpose(0, 3, 2, 1))
    sh["convb"] = pc(inp["conv_dw_b"], 8)
    sh["convlg"] = pc(inp["conv_ln_g"], 8)
    sh["convlb"] = pc(inp["conv_ln_b"], 8)
    sh["bconv"] = pc(inp["b_conv_proj"], KC)
    gwb = np.zeros((DEPTH, 2, 17, 512), f)
    gwb[:, 0, :16] = inp["gla_gate_w_fwd"]
    gwb[:, 0, 16] = inp["gla_gate_b_fwd"]
    gwb[:, 1, :16] = inp["gla_gate_w_bwd"]
    gwb[:, 1, 16] = inp["gla_gate_b_bwd"]
    sh["gwb"] = gwb
    sh["glang"] = pc(inp["gla_norm_g"], 2)
    sh.update(host_consts())
    return sh


def make_in_maps(inp, cores, names=None):
    sh = make_shared_inputs(inp)
    if names is not None:
        sh = {k: v for k, v in sh.items() if k in names}
    maps = []
    for b in cores:
        m = dict(sh)
        m["xT"] = np.ascontiguousarray(np.asarray(inp["x"][b], dtype=np.float32).T.reshape(KC, 128, SEQ))
        maps.append(m)
    return maps


_PROG = {}


def kernel(**inputs):
    inp = {k: np.asarray(v) for k, v in inputs.items()}
    B = inp["x"].shape[0]
    if "full" not in _PROG:
        _PROG["full"] = build_program({})
    P = _PROG["full"]
    maps = make_in_maps(inp, list(range(B)), P.in_names)
    res = run_bass_kernel_spmd(P.nc, maps, core_ids=list(range(B)))
    out = np.empty((B, SEQ, D), np.float32)
    for b in range(B):
        out[b] = res.results[b]["outT"].reshape(D, SEQ).T
    return out


def phase_DA(C, l):
    P, fw, ps, nc = C.P, C.fw, C.ps, C.nc
    lam_init = 0.8 - 0.6 * math.exp(-0.3 * l)
    scale = 128.0 ** -0.5
    with ExitStack() as es:
        lam_t, lam_b = load_small(C, es, "DA_lam", C.lamT[l], [128, 4])
        sg_t, sg_b = load_small(C, es, "DA_sg", C.sublng[l], [128, 2])
        sm = es.enter_context(nc.sbuf_tensor(UN("DA_sm"), [128, 8], F32))
        smb = fw.buf("DA_sm")
        ps.rot = [6, 7]
        fw.op("dve", lambda e: e.tensor_tensor(out=sm[:, 0:1], in0=lam_t[:, 0:1], in1=lam_t[:, 1:2], op=ALU.mult), reads=[lam_b], writes=[smb])
        fw.op("dve", lambda e: e.tensor_tensor(out=sm[:, 1:2], in0=lam_t[:, 2:3], in1=lam_t[:, 3:4], op=ALU.mult), reads=[lam_b, smb], writes=[smb])
        pt, pb = ps.next()
        fw.op("pe", lambda e: e.matmul(pt[:, 0:2], lhsT=C.ones_f[:], rhs=sm[:, 0:2], start=True, stop=True), reads=[smb, C.b_ones], writes=[pb])
        fw.op("act", lambda e: e.activation(out=sm[:, 2:4], in_=pt[:, 0:2], func=AF.Exp), reads=[pb], writes=[smb])
        fw.op("dve", lambda e: e.tensor_tensor(out=sm[:, 4:5], in0=sm[:, 3:4], in1=sm[:, 2:3], op=ALU.subtract), reads=[smb], writes=[smb])
        fw.op("dve", lambda e: e.tensor_scalar(out=sm[:, 5:6], in0=sm[:, 4:5], scalar1=-lam_init, scalar2=None, op0=ALU.add), reads=[smb], writes=[smb])
        fw.op("dve", lambda e: e.tensor_scalar(out=sm[:, 6:8], in0=sg_t[:, 0:2], scalar1=(1.0 - lam_init), scalar2=None, op0=ALU.mult), reads=[smb, sg_b], writes=[smb])
        nlam = sm[:, 5:6]
        q_r = Ring(fw, es, "DA_q", [2, T], BF16, 2)
        k_r = Ring(fw, es, "DA_k", [2, T], BF16, 2)
        v_r = Ring(fw, es, "DA_v", [33, 256], BF16, 2)
        e_r = Ring(fw, es, "DA_e", [512], BF16, 4)
        rd_r = Ring(fw, es, "DA_rd", [512], F32, 2)
        om_r = Ring(fw, es, "DA_om", [2, 512], F32, 4)
        o_r = Ring(fw, es, "DA_o", [2, 512], F32, 2)
        sq_r = Ring(fw, es, "DA_sq", [512], BF16, 2)
        rs_r = Ring(fw, es, "DA_rs", [512], F32, 2)
        st_r = Ring(fw, es, "DA_st", [2, 512], BF16, 3)
        for h in range(4):
            if h not in P.cfg.get("DA_heads", range(4)):
                continue
            qh, qb_ = q_r.next()
            kh, kb_ = k_r.next()
            vh, vb_ = v_r.next()
            fw.dma("sp", qh, C.qT[2 * h:2 * h + 2].rearrange("c p t -> p c t"), writes=[qb_])
            fw.dma("sp", kh, C.kT[2 * h:2 * h + 2].rearrange("c p t -> p c t"), writes=[kb_])
            fw.dma("sp", vh[:NMETA, 0, :], C.vda[0:NMETA, h * 256:(h + 1) * 256], writes=[vb_])
            for i in range(4):
                fw.dma("sp", vh[:, 1 + 8 * i:9 + 8 * i, :], C.vda[NMETA + 1024 * i:NMETA + 1024 * (i + 1), h * 256:(h + 1) * 256].rearrange("(n p) c -> p n c", p=128), owner=vb_)
            vb_.wr = {vb_.semname: vb_.dcnt}
            items = [(bi, m, kt) for bi in range(len(BLOCKS)) if bi in P.cfg.get("DA_blocks", range(9)) for m in range(2) for kt in range(len(TILES))]

            def qk_exp(it):
                bi, m, kt = it
                q0, nq = BLOCKS[bi]
                k0, nk = TILES[kt]
                st, stb = ps.next()
                fw.op("pe", lambda e: e.matmul(st[:nk, :nq], lhsT=kh[:, m, k0:k0 + nk], rhs=qh[:, m, q0:q0 + nq], start=True, stop=True), reads=[kb_, qb_], writes=[stb])
                ea, eb = e_r.next()
                fw.op("act", lambda e: e.activation(out=ea[:nk, :nq], in_=st[:nk, :nq], func=AF.Exp, scale=scale), reads=[stb], writes=[eb])
                return ea, eb

            def epilogue(om0, om0b, om1, om1b, q0, nq):
                o, ob = o_r.next()
                fw.op("dve", lambda e: e.scalar_tensor_tensor(out=o[:, :, :nq], in0=om1[:, :, :nq], scalar=nlam, in1=om0[:, :, :nq], op0=ALU.mult, op1=ALU.add), reads=[om0b, om1b, smb], writes=[ob])
                sp_, spb = ps.next()
                for vc in range(2):
                    sq, sqb = sq_r.next()
                    fw.op("act", lambda e: e.activation(out=sq[:, :nq], in_=o[:, vc, :nq], func=AF.Square), reads=[ob], writes=[sqb])
                    fw.op("pe", lambda e: e.matmul(sp_[:, :nq], lhsT=C.ones_bf[:], rhs=sq[:, :nq], start=(vc == 0), stop=(vc == 1)), reads=[sqb, C.b_ones], writes=[spb])
                rs, rsb = rs_r.next()
                fw.op("act", lambda e: e.activation(out=rs[:, :nq], in_=sp_[:, :nq], func=AF.Sqrt, bias=EPS, scale=1.0 / 256), reads=[spb], writes=[rsb])
                fw.op("dve", lambda e: e.reciprocal(out=rs[:, :nq], in_=rs[:, :nq]), reads=[rsb], writes=[rsb])
                sa, sb = st_r.next()
                for vc in range(2):
                    fw.op("dve", lambda e: e.scalar_tensor_tensor(out=sa[:, vc, :nq], in0=o[:, vc, :nq], scalar=sm[:, 6 + vc:7 + vc], in1=rs[:, :nq], op0=ALU.mult, op1=ALU.mult),
                          reads=[ob, smb, rsb], pwrites=[sb] if vc else (), writes=() if vc else [sb])
                fw.dma("pool", C.odaT[2 * h:2 * h + 2, :, q0:q0 + nq].rearrange("c p t -> p c t"), sa[:, :, :nq], reads=[sb], owner=sb)

            deferred = []
            pend = qk_exp(items[0]) if items else None
            oms = []
            for ii, (bi, m, kt) in enumerate(items):
                while deferred and deferred[0][0] <= ii:
                    epilogue(*deferred.pop(0)[1])
                q0, nq = BLOCKS[bi]
                k0, nk = TILES[kt]
                ea, eb = pend
                if ii + 1 < len(items):
                    pend = qk_exp(items[ii + 1])
                a0, a0b = ps.bank(3 * m)
                a1, a1b = ps.bank(3 * m + 1)
                ad, adb = ps.bank(3 * m + 2)
                first, last = (kt == 0), (kt == len(TILES) - 1)
                fw.op("pe", lambda e: e.matmul(a0[:, :nq], lhsT=vh[:nk, kt, 0:128], rhs=ea[:nk, :nq], start=first, stop=last), reads=[vb_, eb], writes=[a0b])
                fw.op("pe", lambda e: e.matmul(a1[:, :nq], lhsT=vh[:nk, kt, 128:256], rhs=ea[:nk, :nq], start=first, stop=last), reads=[vb_, eb], writes=[a1b])
                fw.op("pe", lambda e: e.matmul(ad[:, :nq], lhsT=C.ones_bf[:nk, :], rhs=ea[:nk, :nq], start=first, stop=last), reads=[C.b_ones, eb], writes=[adb])
                if not last:
                    continue
                rd, rdb = rd_r.next()
                fw.op("act", lambda e: e.copy(out=rd[:, :nq], in_=ad[:, :nq]), reads=[adb], writes=[rdb])
                fw.op("dve", lambda e: e.reciprocal(out=rd[:, :nq], in_=rd[:, :nq]), reads=[rdb], writes=[rdb])
                om, omb = om_r.next()
                fw.op("act", lambda e: e.copy(out=om[:, 0, :nq], in_=a0[:, :nq]), reads=[a0b], writes=[omb])
                fw.op("act", lambda e: e.copy(out=om[:, 1, :nq], in_=a1[:, :nq]), reads=[a1b], pwrites=[omb])
                fw.op("dve", lambda e: e.tensor_tensor(out=om[:, 0, :nq], in0=om[:, 0, :nq], in1=rd[:, :nq], op=ALU.mult), reads=[omb, rdb], writes=[omb])
                fw.op("dve", lambda e: e.tensor_tensor(out=om[:, 1, :nq], in0=om[:, 1, :nq], in1=rd[:, :nq], op=ALU.mult), reads=[omb, rdb], writes=[omb])
                oms.append((om, omb))
                if m == 0:
                    continue
                (om0, om0b), (om1, om1b) = oms
                oms = []
                deferred.append((ii + 6, (om0, om0b, om1, om1b, q0, nq)))
            while deferred:
                epilogue(*deferred.pop(0)[1])
        ps.rot = list(range(8))
        fw.barrier()


def phase_CV(C, l):
    P, fw, ps, nc = C.P, C.fw, C.ps, C.nc
    PAD = 15
    with ExitStack() as es:
        cw_t, cw_b = load_small(C, es, "CV_w", C.convw[l], [128, 8, 31])
        cb_t, cb_b = load_small(C, es, "CV_b", C.convb[l], [128, 8])
        lg_t, lg_b = load_small(C, es, "CV_lg", C.convlg[l], [128, 8])
        lb_t, lb_b = load_small(C, es, "CV_lb", C.convlb[l], [128, 8])
        id_t, id_b = load_small(C, es, "CV_id", C.ident_d, [128, 128])
        z = es.enter_context(nc.sbuf_tensor(UN("CV_z"), [128, 8, T + 2 * PAD], BF16))
        zb = [fw.buf("CV_z") for _ in range(8)]
        Dm = es.enter_context(nc.sbuf_tensor(UN("CV_D"), [128, 8, 31, 128], BF16))
        Db = fw.buf("CV_D")
        with ExitStack() as es2:
            a_r = Ring(fw, es2, "CV_a", [T], BF16, 2)
            g_r = Ring(fw, es2, "CV_g", [T], BF16, 2)
            for c in range(8):
                fw.op("dve", lambda e: e.memset(z[:, c, 0:PAD], 0.0), writes=[zb[c]])
                fw.op("dve", lambda e: e.memset(z[:, c, PAD + T:], 0.0), pwrites=[zb[c]])
                aa, ab = a_r.next()
                ga, gb = g_r.next()
                fw.dma("sp", aa, C.cuA[c], writes=[ab])
                fw.dma("sp", ga, C.cuG[c], writes=[gb])
                fw.op("dve", lambda e: e.tensor_tensor(out=z[:, c, PAD:PAD + T], in0=aa, in1=ga, op=ALU.mult), reads=[ab, gb], pwrites=[zb[c]])
                for j in range(31):
                    fw.op("dve", lambda e: e.tensor_scalar(out=Dm[:, c, j, :], in0=id_t[:], scalar1=cw_t[:, c, j:j + 1], scalar2=None, op0=ALU.mult),
                          reads=[id_b, cw_b], pwrites=[Db])
            fw.barrier()
        xc = es.enter_context(nc.sbuf_tensor(UN("CV_x"), [128, 8, 512], F32))
        xb = fw.buf("CV_x")
        sq_r = Ring(fw, es, "CV_sq", [512], F32, 2)
        mu_r = Ring(fw, es, "CV_mu", [3, 512], F32, 2)
        t_r = Ring(fw, es, "CV_t", [512], F32, 3)
        st_r = Ring(fw, es, "CV_st", [8, 512], BF16, 2)
        ps.rot = [2, 3, 4, 5, 6, 7]
        for bi, (t0, nt) in enumerate(BLOCKS):
            s1, s1b = ps.bank(0)
            s2, s2b = ps.bank(1)
            for c in range(8):
                pt, pb = ps.next()
                for j in range(31):
                    fw.op("pe", lambda e: e.matmul(pt[:, :nt], lhsT=Dm[:, c, j, :], rhs=z[:, c, t0 + j:t0 + j + nt], start=(j == 0), stop=(j == 30)), reads=[Db, zb[c]], writes=[pb])
                fw.op("act", lambda e: e.activation(out=xc[:, c, :nt], in_=pt[:, :nt], func=AF.Identity, bias=cb_t[:, c:c + 1]), reads=[pb, cb_b], pwrites=[xb] if c else (), writes=() if c else [xb])
                sq, sqb = sq_r.next()
                fw.op("act", lambda e: e.activation(out=sq[:, :nt], in_=xc[:, c, :nt], func=AF.Square), reads=[xb], writes=[sqb])
                fw.op("pe", lambda e: e.matmul(s1[:, :nt], lhsT=C.ones_f[:], rhs=xc[:, c, :nt], start=(c == 0), stop=(c == 7)), reads=[xb, C.b_ones], writes=[s1b])
                fw.op("pe", lambda e: e.matmul(s2[:, :nt], lhsT=C.ones_f[:], rhs=sq[:, :nt], start=(c == 0), stop=(c == 7)), reads=[sqb, C.b_ones], writes=[s2b])
            mu, mub = mu_r.next()
            fw.op("act", lambda e: e.activation(out=mu[:, 0, :nt], in_=s1[:, :nt], func=AF.Copy, scale=1.0 / 1024), reads=[s1b], writes=[mub])
            fw.op("dve", lambda e: e.tensor_tensor(out=mu[:, 1, :nt], in0=mu[:, 0, :nt], in1=mu[:, 0, :nt], op=ALU.mult), reads=[mub], writes=[mub])
            fw.op("act", lambda e: e.activation(out=mu[:, 2, :nt], in_=s2[:, :nt], func=AF.Copy, scale=1.0 / 1024), reads=[s2b, mub], writes=[mub])
            fw.op("dve", lambda e: e.tensor_tensor(out=mu[:, 2, :nt], in0=mu[:, 2, :nt], in1=mu[:, 1, :nt], op=ALU.subtract), reads=[mub], writes=[mub])
            fw.op("act", lambda e: e.activation(out=mu[:, 2, :nt], in_=mu[:, 2, :nt], func=AF.Sqrt, bias=EPS, scale=1.0), reads=[mub], writes=[mub])
            fw.op("dve", lambda e: e.reciprocal(out=mu[:, 2, :nt], in_=mu[:, 2, :nt]), reads=[mub], writes=[mub])
            sa, sb = st_r.next()
            for c in range(8):
                ta, tb = t_r.next()
                fw.op("dve", lambda e: e.tensor_tensor(out=ta[:, :nt], in0=xc[:, c, :nt], in1=mu[:, 0, :nt], op=ALU.subtract), reads=[xb, mub], writes=[tb])
                fw.op("dve", lambda e: e.tensor_tensor(out=ta[:, :nt], in0=ta[:, :nt], in1=mu[:, 2, :nt], op=ALU.mult), reads=[tb, mub], writes=[tb])
                fw.op("act", lambda e: e.activation(out=sa[:, c, :nt], in_=ta[:, :nt], func=AF.Silu, bias=lb_t[:, c:c + 1], scale=lg_t[:, c:c + 1]), reads=[tb, lg_b, lb_b],
                      pwrites=[sb] if c else (), writes=() if c else [sb])
            fw.dma("pool", C.ocvT[:, :, t0:t0 + nt].rearrange("c p t -> p c t"), sa[:, :, :nt], reads=[sb], owner=sb)
        ps.rot = list(range(8))
        fw.barrier()


def phase_MRG(C, l):
    P, fw, ps, nc = C.P, C.fw, C.ps, C.nc
    with ExitStack() as es:
        bc_t, bc_b = load_small(C, es, "MG_bc", C.bconv[l], [128, KC])
        br = es.enter_context(nc.sbuf_tensor(UN("MG_br"), [128, 24, 512], BF16))
        brb = fw.buf("MG_br")
        mg = es.enter_context(nc.sbuf_tensor(UN("MG_mg"), [128, KC, 512], BF16))
        mgb = fw.buf("MG_mg")
        hbuf = es.enter_context(nc.sbuf_tensor(UN("MG_h"), [128, KC, 512], F32))
        hb = fw.buf("MG_h")
        w_r = Ring(fw, es, "MG_w", [8192], BF16, 2)
        wp_r = Ring(fw, es, "MG_wp", [3, 4096], BF16, 2)
        gt_r = Ring(fw, es, "MG_gt", [3, 512], BF16, 3)
        m_r = Ring(fw, es, "MG_m", [3, 512], BF16, 3)
        ho_r = Ring(fw, es, "MG_ho", [512], F32, 3)
        gview = C.gatesT.rearrange("(i c) p t -> c p i t", i=3)
        for bi, (t0, nt) in enumerate(BLOCKS):
            fw.dma("sp", hbuf[:, :, :nt], C.h_src(l, bi), writes=[hb])
            fw.dma("sp", br[:, 0:8, :nt], C.odaT[:, :, t0:t0 + nt].rearrange("c p t -> p c t"), writes=[brb])
            fw.dma("sp", br[:, 8:16, :nt], C.ocvT[:, :, t0:t0 + nt].rearrange("c p t -> p c t"), owner=brb)
            fw.dma("sp", br[:, 16:24, :nt], C.oglaT[:, :, t0:t0 + nt].rearrange("c p t -> p c t"), owner=brb)
            brb.wr = {brb.semname: brb.dcnt}
            for g in range(4):
                wa, wb = wp_r.next()
                for i, nm in enumerate(("w_da_proj", "w_conv_proj", "w_gla_proj")):
                    if i == 0:
                        fw.dma("sp", wa[:, i, :], C.wgroup_ap(nm, l, g), writes=[wb])
                    else:
                        fw.dma("sp", wa[:, i, :], C.wgroup_ap(nm, l, g), owner=wb)
                wb.wr = {wb.semname: wb.dcnt}
                for ci in range(4):
                    oc = g * 4 + ci
                    ys = []
                    for i in range(3):
                        pt, pb = ps.next()
                        wv = wa[:, i, :].rearrange("p (k c) -> p k c", c=512)
                        for k in range(8):
                            fw.op("pe", lambda e: e.matmul(pt[:, :nt], lhsT=wv[:, k, ci * 128:(ci + 1) * 128], rhs=br[:, 8 * i + k, :nt], start=(k == 0), stop=(k == 7)), reads=[wb, brb], writes=[pb])
                        ys.append((pt, pb))
                    ga, gb = gt_r.next()
                    fw.dma("sp", ga[:, :, :nt], gview[oc, :, :, t0:t0 + nt], writes=[gb])
                    ma, mb = m_r.next()
                    (yd, ydb), (yc, ycb), (yg, ygb) = ys
                    fw.op("act", lambda e: e.copy(out=ma[:, 0, :nt], in_=yd[:, :nt]), reads=[ydb], writes=[mb])
                    fw.op("act", lambda e: e.activation(out=ma[:, 1, :nt], in_=yc[:, :nt], func=AF.Identity, bias=bc_t[:, oc:oc + 1]), reads=[ycb, bc_b], pwrites=[mb])
                    fw.op("act", lambda e: e.copy(out=ma[:, 2, :nt], in_=yg[:, :nt]), reads=[ygb], pwrites=[mb])
                    fw.op("dve", lambda e: e.tensor_tensor(out=ma[:, :, :nt], in0=ma[:, :, :nt], in1=ga[:, :, :nt], op=ALU.mult), reads=[mb, gb], writes=[mb])
                    fw.op("dve", lambda e: e.tensor_tensor(out=ma[:, 0, :nt], in0=ma[:, 0, :nt], in1=ma[:, 1, :nt], op=ALU.add), reads=[mb], writes=[mb])
                    fw.op("dve", lambda e: e.tensor_tensor(out=mg[:, oc, :nt], in0=ma[:, 0, :nt], in1=ma[:, 2, :nt], op=ALU.add), reads=[mb], pwrites=[mgb] if oc else (), writes=() if oc else [mgb])
            for g in range(4):
                wa, wb = w_r.next()
                fw.dma("sp", wa, C.wgroup_ap("w_out", l, g), writes=[wb])
                wv = wa.rearrange("p (k c) -> p k c", c=512)
                for ci in range(4):
                    oc = g * 4 + ci
                    pt, pb = ps.next()
                    for k in range(KC):
                        fw.op("pe", lambda e: e.matmul(pt[:, :nt], lhsT=wv[:, k, ci * 128:(ci + 1) * 128], rhs=mg[:, k, :nt], start=(k == 0), stop=(k == KC - 1)), reads=[wb, mgb], writes=[pb])
                    ha, hab = ho_r.next()
                    fw.op("act", lambda e: e.copy(out=ha[:, :nt], in_=pt[:, :nt]), reads=[pb], writes=[hab])
                    fw.op("dve", lambda e: e.tensor_tensor(out=ha[:, :nt], in0=ha[:, :nt], in1=hbuf[:, oc, :nt], op=ALU.add), reads=[hab, hb], writes=[hab])
                    fw.dma("pool", C.hT[oc, :, t0:t0 + nt], ha[:, :nt], reads=[hab], owner=hab)
        fw.barrier()


def phase_MLP(C, l):
    P, fw, ps, nc = C.P, C.fw, C.ps, C.nc
    with ExitStack() as es:
        g_t, g_b = load_small(C, es, "ML_g", C.mlpg[l], [128, KC])
        hbuf = es.enter_context(nc.sbuf_tensor(UN("ML_h"), [128, KC, 512], F32))
        hb = fw.buf("ML_h")
        aT = es.enter_context(nc.sbuf_tensor(UN("ML_a"), [128, KC, 512], BF16))
        ab = fw.buf("ML_a")
        fT = es.enter_context(nc.sbuf_tensor(UN("ML_f"), [128, 64, 512], BF16))
        fb = fw.buf("ML_f")
        sqr = Ring(fw, es, "ML_sq", [512], BF16, 3)
        rstd_r = Ring(fw, es, "ML_rstd", [512], F32, 2)
        w_r = Ring(fw, es, "ML_w", [8192], BF16, 2)
        s_r = Ring(fw, es, "ML_s", [512], F32, 2)
        ho_r = Ring(fw, es, "ML_ho", [512], F32, 3)
        for bi, (t0, nt) in enumerate(BLOCKS):
            fw.dma("sp", hbuf[:, :, :nt], C.hT[:, :, t0:t0 + nt].rearrange("c p t -> p c t"), writes=[hb])
            ra, rb = C.rmsnorm_T(hbuf, hb, nt, g_t, None, (sqr, rstd_r))
            for c in range(KC):
                fw.op("dve", lambda e: e.scalar_tensor_tensor(out=aT[:, c, :nt], in0=hbuf[:, c, :nt], scalar=g_t[:, c:c + 1], in1=ra[:, :nt], op0=ALU.mult, op1=ALU.mult),
                      reads=[hb, g_b, rb], pwrites=[ab] if c else (), writes=() if c else [ab])
            for g in range(16):
                wa, wb = w_r.next()
                fw.dma("sp", wa, C.wgroup_ap("w_mlp_in", l, g), writes=[wb])
                wv = wa.rearrange("p (k c) -> p k c", c=512)
                for ci in range(4):
                    fc = g * 4 + ci
                    pt, pb = ps.next()
                    for k in range(KC):
                        fw.op("pe", lambda e: e.matmul(pt[:, :nt], lhsT=wv[:, k, ci * 128:(ci + 1) * 128], rhs=aT[:, k, :nt], start=(k == 0), stop=(k == KC - 1)), reads=[wb, ab], writes=[pb])
                    sa, sb = s_r.next()
                    fw.op("act", lambda e: e.activation(out=sa[:, :nt], in_=pt[:, :nt], func=AF.Relu), reads=[pb], writes=[sb])
                    fw.op("dve", lambda e: e.tensor_tensor(out=fT[:, fc, :nt], in0=sa[:, :nt], in1=sa[:, :nt], op=ALU.mult), reads=[sb],
                          pwrites=[fb] if fc else (), writes=() if fc else [fb])
            for oc in range(16):
                wa, wb = w_r.next()
                fw.dma("sp", wa, C.wgroup_ap("w_mlp_out", l, oc), writes=[wb])
                wv = wa.rearrange("p (k c) -> p k c", c=128)
                pt, pb = ps.next()
                for k in range(64):
                    fw.op("pe", lambda e: e.matmul(pt[:, :nt], lhsT=wv[:, k, :], rhs=fT[:, k, :nt], start=(k == 0), stop=(k == 63)), reads=[wb, fb], writes=[pb])
                ha, hab = ho_r.next()
                fw.op("act", lambda e: e.copy(out=ha[:, :nt], in_=pt[:, :nt]), reads=[pb], writes=[hab])
                fw.op("dve", lambda e: e.tensor_tensor(out=ha[:, :nt], in0=ha[:, :nt], in1=hbuf[:, oc, :nt], op=ALU.add), reads=[hab, hb], writes=[hab])
                fw.dma("pool", C.hT[oc, :, t0:t0 + nt], ha[:, :nt], reads=[hab], owner=hab)
        fw.barrier()


def phase_FIN(C):
    P, fw, ps, nc = C.P, C.fw, C.ps, C.nc
    with ExitStack() as es:
        g_t, g_b = load_small(C, es, "FN_g", C.fing, [128, KC])
        hbuf = es.enter_context(nc.sbuf_tensor(UN("FN_h"), [128, KC, 512], F32))
        hb = fw.buf("FN_h")
        ob = es.enter_context(nc.sbuf_tensor(UN("FN_o"), [128, KC, 512], F32))
        obb = fw.buf("FN_o")
        sqr = Ring(fw, es, "FN_sq", [512], BF16, 3)
        rstd_r = Ring(fw, es, "FN_rstd", [512], F32, 2)
        for bi, (t0, nt) in enumerate(BLOCKS):
            if bi == 0:
                continue
            fw.dma("sp", hbuf[:, :, :nt], C.hT[:, :, t0:t0 + nt].rearrange("c p t -> p c t"), writes=[hb])
            ra, rb = C.rmsnorm_T(hbuf, hb, nt, g_t, None, (sqr, rstd_r))
            for c in range(KC):
                fw.op("dve", lambda e: e.scalar_tensor_tensor(out=ob[:, c, :nt], in0=hbuf[:, c, :nt], scalar=g_t[:, c:c + 1], in1=ra[:, :nt], op0=ALU.mult, op1=ALU.mult),
                      reads=[hb, g_b, rb], pwrites=[obb] if c else (), writes=() if c else [obb])
            fw.dma("pool", C.outT[:, :, t0 - NMETA:t0 - NMETA + nt].rearrange("c p t -> p c t"), ob[:, :, :nt], reads=[obb], owner=obb)
        fw.barrier()


def phase_GLA(C, l):
    P, fw, ps, nc = C.P, C.fw, C.ps, C.nc
    qs = 128.0 ** -0.5
    with ExitStack() as es:
        gm, gmb = load_small(C, es, "GL_gm", C.gmat_d, [128, 6, 128], BF16)
        ci, cib = load_small(C, es, "GL_ci", C.cind_d, [128, 2], BF16)
        ng, ngb = load_small(C, es, "GL_ng", C.glang[l], [128, 2])
        lr = es.enter_context(nc.sbuf_tensor(UN("GL_lr"), [17, 2, T], F32))
        lrb = fw.buf("GL_lr")
        fw.op("dve", lambda e: e.memset(lr[:, :, :], 1.0), writes=[lrb])
        fw.dma("sp", lr[0:16, 0, :], C.glrT[0:16, :], reads=[lrb], owner=lrb)
        fw.dma("sp", lr[0:16, 1, :], C.glrT[16:32, :], owner=lrb)
        lrb.wr = {lrb.semname: lrb.dcnt}
        gw, gwb_ = load_small(C, es, "GL_gw", C.gwb[l].rearrange("d r c -> r d c"), [17, 2, 512])
        of = es.enter_context(nc.sbuf_tensor(UN("GL_of"), [128, 2, T], F32))
        ofb = fw.buf("GL_of")
        qh = es.enter_context(nc.sbuf_tensor(UN("GL_q"), [128, T], BF16))
        kh = es.enter_context(nc.sbuf_tensor(UN("GL_k"), [128, T], BF16))
        ktok = es.enter_context(nc.sbuf_tensor(UN("GL_kt"), [128, 33, 128], BF16))
        vtok = es.enter_context(nc.sbuf_tensor(UN("GL_vt"), [128, 33, 256], BF16))
        e1all = es.enter_context(nc.sbuf_tensor(UN("GL_e1"), [128, 33, 128], F32))
        graw = es.enter_context(nc.sbuf_tensor(UN("GL_gr"), [128, 33, 128], BF16))
        qhb, khb, ktb, vtb = fw.buf("GLq"), fw.buf("GLk"), fw.buf("GLkt"), fw.buf("GLvt")
        e1b, grb = fw.buf("GLe1"), fw.buf("GLgr")
        S = es.enter_context(nc.sbuf_tensor(UN("GL_S"), [128, 256], F32))
        Sbf = es.enter_context(nc.sbuf_tensor(UN("GL_Sbf"), [128, 256], BF16))
        Sb, Sbfb = fw.buf("GLS"), fw.buf("GLSbf")
        f_r = Ring(fw, es, "GL_f", [128], F32, 9)
        b_r = Ring(fw, es, "GL_b", [128], BF16, 12)
        E_r = Ring(fw, es, "GL_E", [2], F32, 4)
        sq_r = Ring(fw, es, "GL_sq", [512], F32, 2)
        rs_r = Ring(fw, es, "GL_rs", [512], F32, 2)
        tm_r = Ring(fw, es, "GL_tm", [512], F32, 2)
        sr_r = Ring(fw, es, "GL_sr", [2, 512], BF16, 2)
        st_r = Ring(fw, es, "GL_st", [2, 512], BF16, 2)
        for h in range(4):
            if h not in P.cfg.get("GLA_heads", range(4)):
                continue
            fw.dma("sp", qh[:, :], C.gqT[h], writes=[qhb])
            fw.dma("sp", kh[:, :], C.gkT[h], writes=[khb])
            fw.dma("sp", ktok[:NMETA, 0, :], C.gktok[0:NMETA, h * 128:(h + 1) * 128], writes=[ktb])
            fw.dma("sp", vtok[:NMETA, 0, :], C.gvtok[0:NMETA, h * 256:(h + 1) * 256], writes=[vtb])
            for i in range(4):
                fw.dma("sp", ktok[:, 1 + 8 * i:9 + 8 * i, :], C.gktok[NMETA + 1024 * i:NMETA + 1024 * (i + 1), h * 128:(h + 1) * 128].rearrange("(n p) c -> p n c", p=128), owner=ktb)
                fw.dma("sp", vtok[:, 1 + 8 * i:9 + 8 * i, :], C.gvtok[NMETA + 1024 * i:NMETA + 1024 * (i + 1), h * 256:(h + 1) * 256].rearrange("(n p) c -> p n c", p=128), owner=vtb)
            ktb.wr = {ktb.semname: ktb.dcnt}
            vtb.wr = {vtb.semname: vtb.dcnt}
            for d in range(2):
                px, pxb = ps.next()
                fw.op("pe", lambda e: e.matmul(px[:NMETA, :128], lhsT=lr[:, d, 0:NMETA], rhs=gw[:, d, h * 128:(h + 1) * 128], start=True, stop=True), reads=[lrb, gwb_], writes=[pxb])
                fw.op("act", lambda e: e.activation(out=e1all[:NMETA, 0, :], in_=px[:NMETA, :128], func=AF.Exp, scale=-1.0), reads=[pxb], writes=[e1b])
                for g in range(8):
                    px, pxb = ps.next()
                    for j in range(4):
                        t0 = NMETA + 128 * (4 * g + j)
                        fw.op("pe", lambda e: e.matmul(px[:, j * 128:(j + 1) * 128], lhsT=lr[:, d, t0:t0 + 128], rhs=gw[:, d, h * 128:(h + 1) * 128], start=True, stop=True), reads=[lrb, gwb_], writes=[pxb])
                    fw.op("act", lambda e: e.activation(out=e1all[:, 1 + 4 * g:5 + 4 * g, :], in_=px[:, :].rearrange("p (a b) -> p a b", b=128), func=AF.Exp, scale=-1.0), reads=[pxb], pwrites=[e1b])
                fw.op("act", lambda e: e.activation(out=graw[:NMETA, 0, :], in_=e1all[:NMETA, 0, :], func=AF.Ln, bias=1.0), reads=[e1b], writes=[grb])
                fw.op("act", lambda e: e.activation(out=graw[:, 1:33, :], in_=e1all[:, 1:33, :], func=AF.Ln, bias=1.0), reads=[e1b], pwrites=[grb])
                fw.op("dve", lambda e: e.memset(S[:, :], 0.0), writes=[Sb])
                fw.op("dve", lambda e: e.memset(Sbf[:, :], 0.0), writes=[Sbfb])
                order = list(enumerate(TILES))
                if d == 1:
                    order = order[::-1]

                def pre(ti, t0, n):
                    gr = graw[:, ti, :]
                    pb_, pbb = ps.next()
                    fw.op("pe", lambda e: e.matmul(pb_[:, :n], lhsT=gr[:n, :], rhs=gm[:n, d, :n], start=True, stop=True), reads=[grb, gmb], writes=[pbb])
                    pr, prb = ps.next()
                    fw.op("pe", lambda e: e.matmul(pr[:n, :128], lhsT=gm[:n, 2 + d, :n], rhs=gr[:n, :], start=True, stop=True), reads=[grb, gmb], writes=[prb])
                    pc, pcb = ps.next()
                    fw.op("pe", lambda e: e.matmul(pc[:, 0:2], lhsT=gr[:n, :], rhs=ci[:n, 0:2], start=True, stop=True), reads=[grb, cib], writes=[pcb])
                    eb, ebb = f_r.next()
                    fw.op("act", lambda e: e.activation(out=eb[:, :n], in_=pb_[:, :n], func=AF.Exp), reads=[pbb], writes=[ebb])
                    enb, enbb = f_r.next()
                    fw.op("act", lambda e: e.activation(out=enb[:, :n], in_=pb_[:, :n], func=AF.Exp, scale=-1.0), reads=[pbb], writes=[enbb])
                    er, erb = f_r.next()
                    fw.op("act", lambda e: e.activation(out=er[:n, :], in_=pr[:n, :128], func=AF.Exp), reads=[prb], writes=[erb])
                    E, Eb = E_r.next()
                    fw.op("act", lambda e: e.activation(out=E[:, 0:2], in_=pc[:, 0:2], func=AF.Exp), reads=[pcb], writes=[Eb])
                    qt, qtb = b_r.next()
                    fw.op("dve", lambda e: e.scalar_tensor_tensor(out=qt[:, :n], in0=qh[:, t0:t0 + n], scalar=qs, in1=eb[:, :n], op0=ALU.mult, op1=ALU.mult), reads=[qhb, ebb], writes=[qtb])
                    kt, ktb2 = b_r.next()
                    fw.op("dve", lambda e: e.tensor_tensor(out=kt[:, :n], in0=kh[:, t0:t0 + n], in1=enb[:, :n], op=ALU.mult), reads=[khb, enbb], writes=[ktb2])
                    kha, khab = b_r.next()
                    fw.op("dve", lambda e: e.tensor_tensor(out=kha[:n, :], in0=ktok[:n, ti, :], in1=er[:n, :], op=ALU.mult), reads=[ktb, erb], writes=[khab])
                    pa, pab = ps.next()
                    fw.op("pe", lambda e: e.matmul(pa[:n, :n], lhsT=kt[:, :n], rhs=qt[:, :n], start=True, stop=True), reads=[ktb2, qtb], writes=[pab])
                    atm, atmb = b_r.next()
                    fw.op("dve", lambda e: e.tensor_tensor(out=atm[:n, :n], in0=pa[:n, :n], in1=gm[:n, 4 + d, :n], op=ALU.mult), reads=[pab, gmb], writes=[atmb])
                    return (qt, qtb, kha, khab, atm, atmb, E, Eb)

                def scan(ti, t0, n, hd):
                    qt, qtb, kha, khab, atm, atmb, E, Eb = hd
                    os_ = [ps.next(), ps.next()]
                    for vc in range(2):
                        o, ob_ = os_[vc]
                        fw.op("pe", lambda e: e.matmul(o[:, :n], lhsT=vtok[:n, ti, vc * 128:(vc + 1) * 128], rhs=atm[:n, :n], start=True, stop=False), reads=[vtb, atmb], writes=[ob_])
                    chunks = [(0, n, 0)] if n <= 64 else [(0, 64, 0), (64, 128, 1)]
                    if d == 1:
                        chunks = chunks[::-1]
                    for cidx_i, (c0, c1, cx) in enumerate(chunks):
                        lastc = cidx_i == len(chunks) - 1
                        for vc in range(2):
                            o, ob_ = os_[vc]
                            fw.op("pe", lambda e: e.matmul(o[:, c0:c1], lhsT=Sbf[:, vc * 128:(vc + 1) * 128], rhs=qt[:, c0:c1], start=False, stop=lastc), reads=[Sbfb, qtb], writes=[ob_])
                        pS, pSb = ps.next()
                        fw.op("pe", lambda e: e.matmul(pS[:, :256], lhsT=kha[c0:c1, :], rhs=vtok[c0:c1, ti, :], start=True, stop=True), reads=[khab, vtb], writes=[pSb])
                        fw.op("dve", lambda e: e.scalar_tensor_tensor(out=Sbf[:, :], in0=S[:, :], scalar=E[:, cx:cx + 1], in1=pS[:, :256], op0=ALU.mult, op1=ALU.add), reads=[Sb, Eb, pSb], writes=[Sbfb])
                        fw.op("dve", lambda e: e.scalar_tensor_tensor(out=S[:, :], in0=S[:, :], scalar=E[:, cx:cx + 1], in1=pS[:, :256], op0=ALU.mult, op1=ALU.add), reads=[Sb, Eb, pSb], writes=[Sb])
                    for vc in range(2):
                        o, ob_ = os_[vc]
                        if d == 0:
                            fw.op("act", lambda e: e.copy(out=of[:, vc, t0:t0 + n], in_=o[:, :n]), reads=[ob_], pwrites=[ofb])
                        else:
                            fw.op("dve", lambda e: e.tensor_tensor(out=of[:, vc, t0:t0 + n], in0=o[:, :n], in1=of[:, vc, t0:t0 + n], op=ALU.add), reads=[ob_, ofb], pwrites=[ofb])

                hd = pre(order[0][0], *order[0][1])
                for oi, (ti, (t0, n)) in enumerate(order):
                    hn = pre(order[oi + 1][0], *order[oi + 1][1]) if oi + 1 < len(order) else None
                    scan(ti, t0, n, hd)
                    hd = hn
            for bi, (t0, nt) in enumerate(BLOCKS):
                pn, pnb = ps.next()
                for vc in range(2):
                    sq, sqb = sq_r.next()
                    fw.op("act", lambda e: e.activation(out=sq[:, :nt], in_=of[:, vc, t0:t0 + nt], func=AF.Square), reads=[ofb], writes=[sqb])
                    fw.op("pe", lambda e: e.matmul(pn[:, :nt], lhsT=C.ones_f[:], rhs=sq[:, :nt], start=(vc == 0), stop=(vc == 1)), reads=[sqb, C.b_ones], writes=[pnb])
                rs, rsb = rs_r.next()
                fw.op("act", lambda e: e.activation(out=rs[:, :nt], in_=pn[:, :nt], func=AF.Sqrt, bias=EPS, scale=1.0 / 256), reads=[pnb], writes=[rsb])
                fw.op("dve", lambda e: e.reciprocal(out=rs[:, :nt], in_=rs[:, :nt]), reads=[rsb], writes=[rsb])
                sr, srb = sr_r.next()
                fw.dma("sp", sr[:, :, :nt], C.srT[2 * h:2 * h + 2, :, t0:t0 + nt].rearrange("c p t -> p c t"), writes=[srb])
                sa, sb = st_r.next()
                for vc in range(2):
                    tm, tmb = tm_r.next()
                    fw.op("dve", lambda e: e.scalar_tensor_tensor(out=tm[:, :nt], in0=of[:, vc, t0:t0 + nt], scalar=ng[:, vc:vc + 1], in1=rs[:, :nt], op0=ALU.mult, op1=ALU.mult), reads=[ofb, ngb, rsb], writes=[tmb])
                    fw.op("dve", lambda e: e.tensor_tensor(out=sa[:, vc, :nt], in0=tm[:, :nt], in1=sr[:, vc, :nt], op=ALU.mult), reads=[tmb, srb], pwrites=[sb] if vc else (), writes=() if vc else [sb])
                fw.dma("pool", C.oglaT[2 * h:2 * h + 2, :, t0:t0 + nt].rearrange("c p t -> p c t"), sa[:, :, :nt], reads=[sb], owner=sb)
        fw.barrier()
```

```python
import math
from contextlib import ExitStack
import numpy as np
import ml_dtypes
import concourse.bass as bass
import concourse.mybir as mybir
from concourse.bass_utils import run_bass_kernel_spmd

F32 = mybir.dt.float32
BF16 = mybir.dt.bfloat16
AF = mybir.ActivationFunctionType
ALU = mybir.AluOpType

D = 2048
SEQ = 4096
NMETA = 16
T = SEQ + NMETA
DEPTH = 2
KC = D // 128
EPS = 1e-6
DFF = 4 * D
IN_SIZES = (1024, 1024, 1024, 2048, 512, 512, 1024, 32, 1024, 6144)
IN_W = sum(IN_SIZES)
BLOCKS = [(0, NMETA)] + [(NMETA + 512 * i, 512) for i in range(8)]
TILES = [(0, NMETA)] + [(NMETA + 128 * i, 128) for i in range(32)]


def block_tiles(bi):
    t0, nt = BLOCKS[bi]
    if nt <= 128:
        return [(t0, nt)]
    return [(t0 + 128 * i, 128) for i in range(nt // 128)]


_UNC = [0]


def UN(name):
    _UNC[0] += 1
    return f"{name}_{_UNC[0]}"


class Buf:
    __slots__ = ("name", "wr", "rd", "sem", "semname", "dcnt")

    def __init__(self, name):
        self.name = name
        self.wr = {}
        self.rd = {}
        self.sem = None
        self.semname = None
        self.dcnt = 0


class FW:
    def __init__(self, nc, es):
        self.nc = nc
        self.es = es
        self.eng = {"pe": nc.tensor, "act": nc.scalar, "dve": nc.vector, "pool": nc.gpsimd, "sp": nc.sync}
        self.sems = {}
        self.latest = {}
        self.ecnt = {}
        for k in self.eng:
            nm = "e_" + k
            self.sems[nm] = es.enter_context(nc.semaphore(nm))
            self.latest[nm] = 0
            self.ecnt[k] = 0
        self.waited = {k: {} for k in self.eng}
        self.nbuf = 0
        self.ninst = 0
        self.free = []
        self.owners = []
        self.semval = {}

    def buf(self, name="b"):
        self.nbuf += 1
        return Buf(f"{name}{self.nbuf}")

    def _need(self, reads, writes, pwrites):
        need = {}
        for b in reads:
            for k, v in b.wr.items():
                if need.get(k, 0) < v:
                    need[k] = v
        for b in writes:
            for dct in (b.wr, b.rd):
                for k, v in dct.items():
                    if need.get(k, 0) < v:
                        need[k] = v
        for b in pwrites:
            for k, v in b.rd.items():
                if need.get(k, 0) < v:
                    need[k] = v
        return need

    def _waits(self, e, need):
        w = self.waited[e]
        own = "e_" + e
        for k, v in need.items():
            if w.get(k, 0) >= v:
                continue
            if k == own:
                if e == "pe" or v < self.ecnt[e] - 1:
                    continue
            self.eng[e].wait_ge(self.sems[k], v)
            w[k] = v

    def _record(self, key, val, reads, writes, pwrites):
        self.latest[key] = val
        for b in reads:
            if b.rd.get(key, 0) < val:
                b.rd[key] = val
        for b in writes:
            b.wr = {key: val}
            b.rd = {}
        for b in pwrites:
            if b.wr.get(key, 0) < val:
                b.wr[key] = val

    def op(self, e, fn, reads=(), writes=(), pwrites=()):
        self._waits(e, self._need(reads, writes, pwrites))
        inst = fn(self.eng[e])
        self.ecnt[e] += 1
        key = "e_" + e
        inst.then_inc(self.sems[key], 1)
        self._record(key, self.ecnt[e], reads, writes, pwrites)
        self.ninst += 1
        return inst

    def dma(self, q, out, in_, reads=(), writes=(), owner=None, **kw):
        self._waits(q, self._need(reads, writes, ()))
        if owner is None:
            owner = writes[0] if writes else reads[0]
        if owner.sem is None:
            if self.free:
                owner.semname = self.free.pop()
            else:
                owner.semname = "d_" + owner.name
                self.sems[owner.semname] = self.es.enter_context(self.nc.semaphore(owner.semname))
                self.semval[owner.semname] = 0
            owner.sem = self.sems[owner.semname]
            self.owners.append(owner)
        inst = self.eng[q].dma_start(out=out, in_=in_, **kw)
        owner.dcnt = self.semval[owner.semname] + 16
        self.semval[owner.semname] = owner.dcnt
        inst.then_inc(owner.sem, 16)
        self._record(owner.semname, owner.dcnt, reads, writes, ())
        self.ninst += 1
        return inst

    def barrier(self, engines=None):
        for e in (engines or self.eng):
            self._waits_all(e)
        for b in self.owners:
            self.free.append(b.semname)
            b.sem = None
        self.owners = []

    def _waits_all(self, e):
        w = self.waited[e]
        for k, v in self.latest.items():
            if v > w.get(k, 0) and k != "e_" + e:
                self.eng[e].wait_ge(self.sems[k], v)
                w[k] = v


class Ring:
    def __init__(self, fw, es, name, shape, dtype, n):
        self.t = es.enter_context(fw.nc.sbuf_tensor(UN(name), [128, n] + list(shape), dtype))
        self.bufs = [fw.buf(name) for _ in range(n)]
        self.n = n
        self.i = 0

    def next(self):
        j = self.i % self.n
        self.i += 1
        return self.t[:, j], self.bufs[j]


class PsumPool:
    def __init__(self, fw, es, n=8):
        self.ts = [es.enter_context(fw.nc.psum_tensor(f"ps{i}", [128, 512], F32)) for i in range(n)]
        self.bufs = [fw.buf("ps") for _ in range(n)]
        self.n = n
        self.i = 0
        self.rot = list(range(n))

    def next(self):
        j = self.rot[self.i % len(self.rot)]
        self.i += 1
        return self.ts[j], self.bufs[j]

    def bank(self, j):
        return self.ts[j], self.bufs[j]


def w_in_groups():
    gs = []
    col = 0
    for s, w in enumerate(IN_SIZES):
        gc = 512 if w >= 512 else w
        for j in range(w // gc):
            gs.append((s, col + j * gc, gc, j))
        col += w
    return gs


W_SPECS = {
    "w_in": (D, IN_W, None),
    "w_da_proj": (1024, D, 512),
    "w_conv_proj": (1024, D, 512),
    "w_gla_proj": (1024, D, 512),
    "w_out": (D, D, 512),
    "w_mlp_in": (D, DFF, 512),
    "w_mlp_out": (DFF, D, 128),
}


def w_groups(name):
    K, M, gc = W_SPECS[name]
    if name == "w_in":
        return [(c0, g) for (_, c0, g, _) in w_in_groups()]
    return [(c0, gc) for c0 in range(0, M, gc)]


def w_scratch_elems(name):
    K, M, _ = W_SPECS[name]
    return K * M


class BgCast:
    def __init__(self, C, es):
        self.C = C
        self.fw = C.fw
        self.st = Ring(C.fw, es, "bg_st", [2048], F32, 2)
        self.bo = Ring(C.fw, es, "bg_bf", [2048], BF16, 2)
        self.tasks = []
        self.loaded = []
        self.cast = []
        self.acc = 0.0
        self.interval = 1e9

    def add(self, l, names):
        C = self.C
        for name in names:
            K, M, _ = W_SPECS[name]
            KCn = K // 128
            for gi, (c0, gc) in enumerate(w_groups(name)):
                kcs = min(KCn, 2048 // gc)
                gap = C.wgroup_ap(name, l, gi)
                for kc0 in range(0, KCn, kcs):
                    src = C.Wd[name][l, kc0 * 128:(kc0 + kcs) * 128, c0:c0 + gc].rearrange("(kc p) c -> p kc c", p=128)
                    n = kcs * gc
                    self.tasks.append((name, l, src, gap[:, kc0 * gc:kc0 * gc + n], n, gc))

    def step(self):
        fw = self.fw
        if self.cast:
            ba, bb, t = self.cast.pop(0)
            fw.dma("pool", t[3], ba[:, :t[4]], reads=[bb], owner=bb)
        if self.loaded:
            sa, sb, t = self.loaded.pop(0)
            ba, bb = self.bo.next()
            n = t[4]
            fw.op("pool", lambda e: e.tensor_copy(out=ba[:, :n], in_=sa[:, :n]), reads=[sb], writes=[bb])
            self.cast.append((ba, bb, t))
        if self.tasks:
            t = self.tasks.pop(0)
            sa, sb = self.st.next()
            fw.dma("sp", sa[:, :t[4]].rearrange("p (kc c) -> p kc c", c=t[5]), t[2], writes=[sb])
            self.loaded.append((sa, sb, t))

    def pending(self):
        return bool(self.tasks or self.loaded or self.cast)

    def tick(self, us):
        self.acc += us
        while self.acc >= self.interval and self.pending():
            self.acc -= self.interval
            self.step()
        if not self.pending():
            self.acc = 0.0

    def flush(self, pred):
        last = -1
        for i, t in enumerate(self.tasks):
            if pred(t[0], t[1]):
                last = i
        need_drain = last >= 0 or any(pred(t[0], t[1]) for _, _, t in self.loaded + self.cast)
        for _ in range(last + 1):
            self.step()
        if need_drain:
            while self.loaded or self.cast:
                saved, self.tasks = self.tasks, []
                self.step()
                self.tasks = saved


class Prog:
    def __init__(self, cfg):
        self.cfg = cfg
        self.nc = bass.Bass("TRN2", target_bir_lowering=False)
        self.dbg_outs = []
        self.in_names = []

    def din(self, name, shape, dtype=F32):
        self.in_names.append(name)
        return self.nc.dram_tensor(name, list(shape), dtype, kind="ExternalInput").ap()

    def dscr(self, name, shape, dtype):
        kind = "ExternalOutput" if name in self.cfg.get("dump", ()) else "Internal"
        if kind == "ExternalOutput":
            self.dbg_outs.append(name)
        return self.nc.dram_tensor(name, list(shape), dtype, kind=kind).ap()


def flat_w_offsets(name):
    K, M, _ = W_SPECS[name]
    offs = []
    o = 0
    for (c0, gc) in w_groups(name):
        offs.append(o)
        o += K * gc
    return offs


def build_program(cfg):
    P = Prog(cfg)
    nc = P.nc
    phases = cfg.get("phases", None)
    layers = cfg.get("layers", list(range(DEPTH)))

    def on(ph):
        return phases is None or ph in phases

    xT = P.din("xT", [KC, 128, SEQ])
    metaT = P.din("metaT", [KC, 128, NMETA])
    Wd = {n: P.din(n, [DEPTH, W_SPECS[n][0], W_SPECS[n][1]]) for n in cfg.get("wcast", list(W_SPECS))}
    mixg = P.din("mixg", [DEPTH, 128, KC])
    mlpg = P.din("mlpg", [DEPTH, 128, KC])
    fing = P.din("fing", [128, KC])
    lamT = P.din("lamT", [DEPTH, 128, 4])
    sublng = P.din("sublng", [DEPTH, 128, 2])
    convw = P.din("convw", [DEPTH, 128, 8, 31])
    convb = P.din("convb", [DEPTH, 128, 8])
    convlg = P.din("convlg", [DEPTH, 128, 8])
    convlb = P.din("convlb", [DEPTH, 128, 8])
    bconv = P.din("bconv", [DEPTH, 128, KC])
    gwb = P.din("gwb", [DEPTH, 2, 17, 512])
    glang = P.din("glang", [DEPTH, 128, 2])
    cosT = P.din("cosT", [128, T])
    sinT = P.din("sinT", [128, T])
    perm_d = P.din("perm", [128, 128], BF16)
    ident_d = P.din("ident", [128, 128])
    gmat_d = P.din("gmat", [128, 6, 128], BF16)
    cind_d = P.din("cind", [128, 2], BF16)

    outT = nc.dram_tensor("outT", [KC, 128, SEQ], F32, kind="ExternalOutput").ap()

    wscr = {}
    for n in W_SPECS:
        for l in range(DEPTH):
            wscr[(n, l)] = P.dscr(f"wb_{n}_{l}", [w_scratch_elems(n)], BF16)
    hT = P.dscr("hT", [KC, 128, T], F32)
    qT = P.dscr("qT", [8, 128, T], BF16)
    kT = P.dscr("kT", [8, 128, T], BF16)
    vda = P.dscr("vda", [T, 1024], BF16)
    cuA = P.dscr("cuA", [8, 128, T], BF16)
    cuG = P.dscr("cuG", [8, 128, T], BF16)
    gqT = P.dscr("gqT", [4, 128, T], BF16)
    gkT = P.dscr("gkT", [4, 128, T], BF16)
    gktok = P.dscr("gktok", [T, 512], BF16)
    gvtok = P.dscr("gvtok", [T, 1024], BF16)
    glrT = P.dscr("glrT", [32, T], F32)
    srT = P.dscr("srT", [8, 128, T], BF16)
    gatesT = P.dscr("gatesT", [48, 128, T], BF16)
    odaT = P.dscr("odaT", [8, 128, T], BF16)
    ocvT = P.dscr("ocvT", [8, 128, T], BF16)
    oglaT = P.dscr("oglaT", [8, 128, T], BF16)

    es0 = ExitStack()
    with es0:
        fw = FW(nc, es0)
        ps = PsumPool(fw, es0)
        ones_bf = es0.enter_context(nc.sbuf_tensor(UN("ones_bf"), [128, 128], BF16))
        ones_f = es0.enter_context(nc.sbuf_tensor(UN("ones_f"), [128, 128], F32))
        b_ones = fw.buf("ones")
        fw.op("pool", lambda e: e.memset(ones_bf[:], 1.0), writes=[b_ones])
        fw.op("pool", lambda e: e.memset(ones_f[:], 1.0), writes=[b_ones])
        b_ones.rd = {}

        def h_src(l, bi):
            t0, nt = BLOCKS[bi]
            if l == 0:
                if bi == 0:
                    return metaT.rearrange("c p t -> p c t")
                return xT[:, :, t0 - NMETA:t0 - NMETA + nt].rearrange("c p t -> p c t")
            return hT[:, :, t0:t0 + nt].rearrange("c p t -> p c t")

        def wgroup_ap(name, l, gi):
            K = W_SPECS[name][0]
            c0, gc = w_groups(name)[gi]
            off = flat_w_offsets(name)[gi]
            return wscr[(name, l)][off:off + K * gc].rearrange("(p r) -> p r", p=128)

        def phase_wcast(l, names):
            with ExitStack() as es:
                st = Ring(fw, es, "wc_st", [4096], F32, 3)
                bo = Ring(fw, es, "wc_bf", [4096], BF16, 3)
                k = 0
                for name in names:
                    K, M, _ = W_SPECS[name]
                    KCn = K // 128
                    for gi, (c0, gc) in enumerate(w_groups(name)):
                        kcs = min(KCn, 4096 // gc)
                        gap = wgroup_ap(name, l, gi)
                        for kc0 in range(0, KCn, kcs):
                            sa, sb = st.next()
                            ba, bb = bo.next()
                            src = Wd[name][l, kc0 * 128:(kc0 + kcs) * 128, c0:c0 + gc].rearrange("(kc p) c -> p kc c", p=128)
                            n = kcs * gc
                            fw.dma("sp", sa[:, :n].rearrange("p (kc c) -> p kc c", c=gc), src, writes=[sb])
                            if k % 2 == 0:
                                fw.op("act", lambda e: e.copy(out=ba[:, :n], in_=sa[:, :n]), reads=[sb], writes=[bb])
                            else:
                                fw.op("dve", lambda e: e.tensor_copy(out=ba[:, :n], in_=sa[:, :n]), reads=[sb], writes=[bb])
                            fw.dma("pool", gap[:, kc0 * gc:kc0 * gc + n], ba[:, :n], reads=[bb], owner=bb)
                            k += 1
                fw.barrier()

        def rmsnorm_T(hbuf_ap, hb, nt, g_ap, out_fn, rings):
            sqr, rstd_r = rings
            pt, pb = ps.next()
            for c in range(KC):
                sa, sb = sqr.next()
                fw.op("act", lambda e: e.activation(out=sa[:, :nt], in_=hbuf_ap[:, c, :nt], func=AF.Square), reads=[hb], writes=[sb])
                fw.op("pe", lambda e: e.matmul(pt[:, :nt], lhsT=ones_bf[:], rhs=sa[:, :nt], start=(c == 0), stop=(c == KC - 1)), reads=[sb], writes=[pb])
            ra, rb = rstd_r.next()
            fw.op("act", lambda e: e.activation(out=ra[:, :nt], in_=pt[:, :nt], func=AF.Sqrt, bias=EPS, scale=1.0 / D), reads=[pb], writes=[rb])
            fw.op("dve", lambda e: e.reciprocal(out=ra[:, :nt], in_=ra[:, :nt]), reads=[rb], writes=[rb])
            return ra, rb

        P.fw = fw
        P.ps = ps
        import types
        C = types.SimpleNamespace(**locals())
        build_phases(C)
        fw.barrier()
    return P


def build_phases(C):
    P, fw, ps, nc = C.P, C.fw, C.ps, C.nc
    cfg = P.cfg
    phases = cfg.get("phases", None)
    layers = cfg.get("layers", list(range(DEPTH)))

    def on(ph):
        return phases is None or ph in phases

    OTHER = ["w_da_proj", "w_conv_proj", "w_gla_proj", "w_out", "w_mlp_in", "w_mlp_out"]
    bgmode = cfg.get("bg", True) and phases is None or cfg.get("bg_force", False)
    C.bg = None
    if bgmode:
        C.phase_wcast(layers[0], ["w_in"])
        C.bg = BgCast(C, C.es0)
        C.bg.interval = cfg.get("bg_interval", 22.0)
        C.bg.add(layers[0], OTHER)
        for l in layers[1:]:
            C.bg.add(l, ["w_in"] + OTHER)
    else:
        for l in layers:
            names = cfg.get("wcast", list(W_SPECS))
            if on("wcast"):
                C.phase_wcast(l, names)

    def pre_barrier(ph, l):
        if C.bg is None:
            return
        if ph == "GLA":
            C.bg.flush(lambda n, ll: ll == l and n in ("w_da_proj", "w_conv_proj", "w_gla_proj", "w_out"))
        elif ph == "MRG":
            C.bg.flush(lambda n, ll: ll == l and n in ("w_mlp_in", "w_mlp_out"))
        elif ph == "MLP":
            C.bg.flush(lambda n, ll: ll == l + 1 and n == "w_in")

    def tick(us):
        if C.bg is not None:
            C.bg.tick(us)

    C.pre_barrier = pre_barrier
    C.tick = tick
    for l in layers:
        if on("A"):
            phase_A(C, l)
        if on("DA"):
            phase_DA(C, l)
        if on("CV"):
            phase_CV(C, l)
        if on("GLA"):
            phase_GLA(C, l)
        if on("MRG"):
            phase_MRG(C, l)
        if on("MLP"):
            phase_MLP(C, l)
    if on("FIN"):
        phase_FIN(C)


def load_small(C, es, name, src_ap, shape, dtype=F32, q="sp"):
    t = es.enter_context(C.nc.sbuf_tensor(UN(name), list(shape), dtype))
    b = C.fw.buf(name)
    C.fw.dma(q, t[:], src_ap, writes=[b])
    return t, b


def phase_A(C, l):
    P, fw, ps, nc = C.P, C.fw, C.ps, C.nc
    groups = w_in_groups()
    with ExitStack() as es:
        g_t, g_b = load_small(C, es, "A_g", C.mixg[l], [128, KC])
        perm_t, perm_b = load_small(C, es, "A_perm", C.perm_d, [128, 128], BF16)
        hbuf = es.enter_context(nc.sbuf_tensor(UN("A_h"), [128, KC, 512], F32))
        hb = fw.buf("A_h")
        sqr = Ring(fw, es, "A_sq", [512], BF16, 3)
        rstd_r = Ring(fw, es, "A_rstd", [512], F32, 2)
        aT_r = Ring(fw, es, "A_aT", [KC, 512], BF16, 2)
        w_r = Ring(fw, es, "A_w", [8192], BF16, 2)
        stg_r = Ring(fw, es, "A_stg", [4, 512], BF16, 3)
        stgf_r = Ring(fw, es, "A_stgf", [512], F32, 2)
        cs_r = Ring(fw, es, "A_cs", [2, 512], F32, 2)
        xb_r = Ring(fw, es, "A_xb", [512], BF16, 2)
        t1_r = Ring(fw, es, "A_t1", [512], F32, 2)
        t2_r = Ring(fw, es, "A_t2", [512], F32, 2)
        nev = 0
        for bi, (t0, nt) in enumerate(BLOCKS):
            if bi not in P.cfg.get("A_blocks", range(9)):
                continue
            fw.dma("sp", hbuf[:, :, :nt], C.h_src(l, bi), writes=[hb])
            ra, rb = C.rmsnorm_T(hbuf, hb, nt, g_t, None, (sqr, rstd_r))
            aT, ab = aT_r.next()
            for c in range(KC):
                fw.op("dve", lambda e: e.scalar_tensor_tensor(out=aT[:, c, :nt], in0=hbuf[:, c, :nt], scalar=g_t[:, c:c + 1], in1=ra[:, :nt], op0=ALU.mult, op1=ALU.mult),
                      reads=[hb, g_b, rb], pwrites=[ab] if c else (), writes=() if c else [ab])
            cs, csb = cs_r.next()
            fw.dma("sp", cs[:, 0, :nt], C.cosT[:, t0:t0 + nt], writes=[csb])
            fw.dma("sp", cs[:, 1, :nt], C.sinT[:, t0:t0 + nt], reads=(), writes=(), owner=csb)
            csb.wr = {csb.semname: csb.dcnt}
            tiles = block_tiles(bi)
            for gi, (sec, c0, gc, j) in enumerate(groups):
                if sec not in P.cfg.get("A_secs", range(10)):
                    continue
                wa, wb = w_r.next()
                fw.dma("sp", wa[:, :KC * gc], C.wgroup_ap("w_in", l, gi), writes=[wb])
                C.tick(14.0 * nt / 512.0)
                wv = wa[:, :KC * gc].rearrange("p (k c) -> p k c", c=gc)
                nch = max(1, gc // 128)
                if sec in (2, 5, 6):
                    dst = {2: C.vda, 5: C.gktok, 6: C.gvtok}[sec]
                    dcol = j * gc
                    sa, sb = stg_r.next()
                    for ti, (tt0, tn) in enumerate(tiles):
                        pt, pb = ps.next()
                        for k in range(KC):
                            fw.op("pe", lambda e: e.matmul(pt[:tn, :gc], lhsT=aT[:, k, tt0 - t0:tt0 - t0 + tn], rhs=wv[:, k, :], start=(k == 0), stop=(k == KC - 1)),
                                  reads=[ab, wb], writes=[pb])
                        eng = "act" if nev % 2 == 0 else "dve"
                        nev += 1
                        if eng == "act":
                            fw.op("act", lambda e: e.copy(out=sa[:tn, ti, :gc], in_=pt[:tn, :gc]), reads=[pb], pwrites=[sb] if ti else (), writes=() if ti else [sb])
                        else:
                            fw.op("dve", lambda e: e.tensor_copy(out=sa[:tn, ti, :gc], in_=pt[:tn, :gc]), reads=[pb], pwrites=[sb] if ti else (), writes=() if ti else [sb])
                    ntl = len(tiles)
                    tn = tiles[0][1]
                    dap = dst[t0:t0 + nt, dcol:dcol + gc].rearrange("(n p) c -> p n c", p=tn)
                    fw.dma("pool", dap, sa[:tn, :ntl, :gc], reads=[sb], owner=sb)
                    if sec != 5:
                        continue
                if sec == 7:
                    pt, pb = ps.next()
                    for k in range(KC):
                        fw.op("pe", lambda e: e.matmul(pt[:32, :nt], lhsT=wv[:, k, :], rhs=aT[:, k, :nt], start=(k == 0), stop=(k == KC - 1)), reads=[ab, wb], writes=[pb])
                    sa, sb = stgf_r.next()
                    fw.op("dve", lambda e: e.tensor_copy(out=sa[:32, :nt], in_=pt[:32, :nt]), reads=[pb], writes=[sb])
                    fw.dma("pool", C.glrT[:, t0:t0 + nt], sa[:32, :nt], reads=[sb], owner=sb)
                    continue
                dst, func = {0: (C.qT, None), 1: (C.kT, None), 3: (C.cuA if j < 2 else C.cuG, None if j < 2 else AF.Sigmoid),
                             4: (C.gqT, None), 5: (C.gkT, None), 8: (C.srT, AF.Silu), 9: (C.gatesT, AF.Sigmoid)}[sec]
                jj = (j - 2) if (sec == 3 and j >= 2) else j
                sa, sb = stg_r.next()
                for ci in range(nch):
                    pt, pb = ps.next()
                    for k in range(KC):
                        fw.op("pe", lambda e: e.matmul(pt[:, :nt], lhsT=wv[:, k, ci * 128:(ci + 1) * 128], rhs=aT[:, k, :nt], start=(k == 0), stop=(k == KC - 1)), reads=[ab, wb], writes=[pb])
                    wr = dict(pwrites=[sb] if ci else (), writes=() if ci else [sb])
                    if sec in (0, 1):
                        xa, xbb = xb_r.next()
                        fw.op("act", lambda e: e.copy(out=xa[:, :nt], in_=pt[:, :nt]), reads=[pb], writes=[xbb])
                        p2, p2b = ps.next()
                        fw.op("pe", lambda e: e.matmul(p2[:, :nt], lhsT=perm_t[:], rhs=xa[:, :nt], start=True, stop=True), reads=[perm_b, xbb], writes=[p2b])
                        t1, t1b = t1_r.next()
                        t2, t2b = t2_r.next()
                        if True:
                            fw.op("act", lambda e: e.copy(out=t1[:, :nt], in_=pt[:, :nt]), reads=[pb], writes=[t1b])
                            fw.op("act", lambda e: e.copy(out=t2[:, :nt], in_=p2[:, :nt]), reads=[p2b], writes=[t2b])
                            fw.op("dve", lambda e: e.tensor_tensor(out=t1[:, :nt], in0=t1[:, :nt], in1=cs[:, 0, :nt], op=ALU.mult), reads=[t1b, csb], writes=[t1b])
                            fw.op("dve", lambda e: e.tensor_tensor(out=t2[:, :nt], in0=t2[:, :nt], in1=cs[:, 1, :nt], op=ALU.mult), reads=[t2b, csb], writes=[t2b])
                        else:
                            fw.op("dve", lambda e: e.tensor_tensor(out=t1[:, :nt], in0=pt[:, :nt], in1=cs[:, 0, :nt], op=ALU.mult), reads=[pb, csb], writes=[t1b])
                            fw.op("dve", lambda e: e.tensor_tensor(out=t2[:, :nt], in0=p2[:, :nt], in1=cs[:, 1, :nt], op=ALU.mult), reads=[p2b, csb], writes=[t2b])
                        fw.op("dve", lambda e: e.tensor_tensor(out=sa[:, ci, :nt], in0=t1[:, :nt], in1=t2[:, :nt], op=ALU.add), reads=[t1b, t2b], **wr)
                    elif func is not None:
                        fw.op("act", lambda e: e.activation(out=sa[:, ci, :nt], in_=pt[:, :nt], func=func), reads=[pb], **wr)
                    else:
                        eng = "act" if nev % 2 == 0 else "dve"
                        nev += 1
                        if eng == "act":
                            fw.op("act", lambda e: e.copy(out=sa[:, ci, :nt], in_=pt[:, :nt]), reads=[pb], **wr)
                        else:
                            fw.op("dve", lambda e: e.tensor_copy(out=sa[:, ci, :nt], in_=pt[:, :nt]), reads=[pb], **wr)
                ch0 = jj * nch
                dap = dst[ch0:ch0 + nch, :, t0:t0 + nt].rearrange("c p t -> p c t")
                fw.dma("pool", dap, sa[:, :nch, :nt], reads=[sb], owner=sb)
        fw.barrier()


def host_consts():
    bf = ml_dtypes.bfloat16
    d = 128
    inv_freq = (1.0 / (10000.0 ** (np.arange(0, d, 2, dtype=np.float32) / np.float32(d)))).astype(np.float32)
    ang = (np.arange(T, dtype=np.float32)[:, None] * inv_freq[None, :]).astype(np.float32)
    cos = np.cos(ang).astype(np.float32).T
    sin = np.sin(ang).astype(np.float32).T
    cosT = np.concatenate([cos, cos], 0)
    sinT = np.concatenate([-sin, sin], 0)
    perm = np.zeros((128, 128), np.float32)
    for m in range(128):
        perm[(m + 64) % 128, m] = 1.0
    j = np.arange(128)[:, None]
    i = np.arange(128)[None, :]
    same = (j // 64) == (i // 64)
    s = -1.0 / 16.0
    triF = np.where(same & (j <= i), s, 0.0)
    triB = np.where(same & (j >= i), s, 0.0)
    strF = np.where(same & (j > i), s, 0.0)
    strB = np.where(same & (j < i), s, 0.0)
    maskF = np.where(same & (j <= i), 1.0, 0.0)
    maskB = np.where(same & (j >= i), 1.0, 0.0)
    gmat = np.stack([triF, triB, strF, strB, maskF, maskB], 1).astype(bf)
    cind = np.zeros((128, 2), np.float32)
    cind[:64, 0] = s
    cind[64:, 1] = s
    return dict(cosT=np.ascontiguousarray(cosT), sinT=np.ascontiguousarray(sinT), perm=perm.astype(bf),
                ident=np.eye(128, dtype=np.float32), gmat=np.ascontiguousarray(gmat), cind=cind.astype(bf))


def pc(v, nchunk):
    sh = v.shape[:-1]
    return np.ascontiguousarray(np.swapaxes(v.reshape(sh + (nchunk, 128)), -1, -2))


def make_shared_inputs(inp):
    f = np.float32
    sh = {}
    for n in W_SPECS:
        sh[n] = np.ascontiguousarray(inp[n], dtype=f)
    sh["metaT"] = np.ascontiguousarray(inp["meta_tokens"].T.reshape(KC, 128, NMETA), dtype=f)
    sh["mixg"] = pc(inp["mix_norm_g"], KC)
    sh["mlpg"] = pc(inp["mlp_norm_g"], KC)
    sh["fing"] = pc(inp["final_norm_g"], KC)
    sh["lamT"] = np.ascontiguousarray(np.swapaxes(inp["da_lambda"], 1, 2))
    sh["sublng"] = pc(inp["da_subln_g"], 2)
    sh["convw"] = np.ascontiguousarray(inp["conv_dw_w"].reshape(DEPTH, 31, 8, 128).transpose(0, 3, 2, 1))
    sh["convb"] = pc(inp["conv_dw_b"], 8)
    sh["convlg"] = pc(inp["conv_ln_g"], 8)
    sh["convlb"] = pc(inp["conv_ln_b"], 8)
    sh["bconv"] = pc(inp["b_conv_proj"], KC)
    gwb = np.zeros((DEPTH, 2, 17, 512), f)
    gwb[:, 0, :16] = inp["gla_gate_w_fwd"]
    gwb[:, 0, 16] = inp["gla_gate_b_fwd"]
    gwb[:, 1, :16] = inp["gla_gate_w_bwd"]
    gwb[:, 1, 16] = inp["gla_gate_b_bwd"]
    sh["gwb"] = gwb
    sh["glang"] = pc(inp["gla_norm_g"], 2)
    sh.update(host_consts())
    return sh


def make_in_maps(inp, cores, names=None):
    sh = make_shared_inputs(inp)
    if names is not None:
        sh = {k: v for k, v in sh.items() if k in names}
    maps = []
    for b in cores:
        m = dict(sh)
        m["xT"] = np.ascontiguousarray(np.asarray(inp["x"][b], dtype=np.float32).T.reshape(KC, 128, SEQ))
        maps.append(m)
    return maps


_PROG = {}


def kernel(**inputs):
    inp = {k: np.asarray(v) for k, v in inputs.items()}
    B = inp["x"].shape[0]
    if "full" not in _PROG:
        _PROG["full"] = build_program({})
    P = _PROG["full"]
    maps = make_in_maps(inp, list(range(B)), P.in_names)
    res = run_bass_kernel_spmd(P.nc, maps, core_ids=list(range(B)))
    out = np.empty((B, SEQ, D), np.float32)
    for b in range(B):
        out[b] = res.results[b]["outT"].reshape(D, SEQ).T
    return out


def phase_DA(C, l):
    P, fw, ps, nc = C.P, C.fw, C.ps, C.nc
    lam_init = 0.8 - 0.6 * math.exp(-0.3 * l)
    scale = 128.0 ** -0.5
    with ExitStack() as es:
        lam_t, lam_b = load_small(C, es, "DA_lam", C.lamT[l], [128, 4])
        sg_t, sg_b = load_small(C, es, "DA_sg", C.sublng[l], [128, 2])
        sm = es.enter_context(nc.sbuf_tensor(UN("DA_sm"), [128, 8], F32))
        smb = fw.buf("DA_sm")
        ps.rot = [6, 7]
        fw.op("dve", lambda e: e.tensor_tensor(out=sm[:, 0:1], in0=lam_t[:, 0:1], in1=lam_t[:, 1:2], op=ALU.mult), reads=[lam_b], writes=[smb])
        fw.op("dve", lambda e: e.tensor_tensor(out=sm[:, 1:2], in0=lam_t[:, 2:3], in1=lam_t[:, 3:4], op=ALU.mult), reads=[lam_b, smb], writes=[smb])
        pt, pb = ps.next()
        fw.op("pe", lambda e: e.matmul(pt[:, 0:2], lhsT=C.ones_f[:], rhs=sm[:, 0:2], start=True, stop=True), reads=[smb, C.b_ones], writes=[pb])
        fw.op("act", lambda e: e.activation(out=sm[:, 2:4], in_=pt[:, 0:2], func=AF.Exp), reads=[pb], writes=[smb])
        fw.op("dve", lambda e: e.tensor_tensor(out=sm[:, 4:5], in0=sm[:, 3:4], in1=sm[:, 2:3], op=ALU.subtract), reads=[smb], writes=[smb])
        fw.op("dve", lambda e: e.tensor_scalar(out=sm[:, 5:6], in0=sm[:, 4:5], scalar1=-lam_init, scalar2=None, op0=ALU.add), reads=[smb], writes=[smb])
        fw.op("dve", lambda e: e.tensor_scalar(out=sm[:, 6:8], in0=sg_t[:, 0:2], scalar1=(1.0 - lam_init), scalar2=None, op0=ALU.mult), reads=[smb, sg_b], writes=[smb])
        nlam = sm[:, 5:6]
        q_r = Ring(fw, es, "DA_q", [2, T], BF16, 2)
        k_r = Ring(fw, es, "DA_k", [2, T], BF16, 2)
        v_r = Ring(fw, es, "DA_v", [33, 256], BF16, 2)
        e_r = Ring(fw, es, "DA_e", [512], BF16, 4)
        rd_r = Ring(fw, es, "DA_rd", [512], F32, 2)
        om_r = Ring(fw, es, "DA_om", [2, 512], F32, 4)
        o_r = Ring(fw, es, "DA_o", [2, 512], F32, 2)
        sq_r = Ring(fw, es, "DA_sq", [512], BF16, 2)
        rs_r = Ring(fw, es, "DA_rs", [512], F32, 2)
        st_r = Ring(fw, es, "DA_st", [2, 512], BF16, 3)
        for h in range(4):
            if h not in P.cfg.get("DA_heads", range(4)):
                continue
            qh, qb_ = q_r.next()
            kh, kb_ = k_r.next()
            vh, vb_ = v_r.next()
            fw.dma("sp", qh, C.qT[2 * h:2 * h + 2].rearrange("c p t -> p c t"), writes=[qb_])
            fw.dma("sp", kh, C.kT[2 * h:2 * h + 2].rearrange("c p t -> p c t"), writes=[kb_])
            fw.dma("sp", vh[:NMETA, 0, :], C.vda[0:NMETA, h * 256:(h + 1) * 256], writes=[vb_])
            for i in range(4):
                fw.dma("sp", vh[:, 1 + 8 * i:9 + 8 * i, :], C.vda[NMETA + 1024 * i:NMETA + 1024 * (i + 1), h * 256:(h + 1) * 256].rearrange("(n p) c -> p n c", p=128), owner=vb_)
            vb_.wr = {vb_.semname: vb_.dcnt}
            items = [(bi, m, kt) for bi in range(len(BLOCKS)) if bi in P.cfg.get("DA_blocks", range(9)) for m in range(2) for kt in range(len(TILES))]

            def qk_exp(it):
                bi, m, kt = it
                q0, nq = BLOCKS[bi]
                k0, nk = TILES[kt]
                st, stb = ps.next()
                fw.op("pe", lambda e: e.matmul(st[:nk, :nq], lhsT=kh[:, m, k0:k0 + nk], rhs=qh[:, m, q0:q0 + nq], start=True, stop=True), reads=[kb_, qb_], writes=[stb])
                ea, eb = e_r.next()
                fw.op("act", lambda e: e.activation(out=ea[:nk, :nq], in_=st[:nk, :nq], func=AF.Exp, scale=scale), reads=[stb], writes=[eb])
                return ea, eb

            def epilogue(om0, om0b, om1, om1b, q0, nq):
                o, ob = o_r.next()
                fw.op("dve", lambda e: e.scalar_tensor_tensor(out=o[:, :, :nq], in0=om1[:, :, :nq], scalar=nlam, in1=om0[:, :, :nq], op0=ALU.mult, op1=ALU.add), reads=[om0b, om1b, smb], writes=[ob])
                sp_, spb = ps.next()
                for vc in range(2):
                    sq, sqb = sq_r.next()
                    fw.op("act", lambda e: e.activation(out=sq[:, :nq], in_=o[:, vc, :nq], func=AF.Square), reads=[ob], writes=[sqb])
                    fw.op("pe", lambda e: e.matmul(sp_[:, :nq], lhsT=C.ones_bf[:], rhs=sq[:, :nq], start=(vc == 0), stop=(vc == 1)), reads=[sqb, C.b_ones], writes=[spb])
                rs, rsb = rs_r.next()
                fw.op("act", lambda e: e.activation(out=rs[:, :nq], in_=sp_[:, :nq], func=AF.Sqrt, bias=EPS, scale=1.0 / 256), reads=[spb], writes=[rsb])
                fw.op("dve", lambda e: e.reciprocal(out=rs[:, :nq], in_=rs[:, :nq]), reads=[rsb], writes=[rsb])
                sa, sb = st_r.next()
                for vc in range(2):
                    fw.op("dve", lambda e: e.scalar_tensor_tensor(out=sa[:, vc, :nq], in0=o[:, vc, :nq], scalar=sm[:, 6 + vc:7 + vc], in1=rs[:, :nq], op0=ALU.mult, op1=ALU.mult),
                          reads=[ob, smb, rsb], pwrites=[sb] if vc else (), writes=() if vc else [sb])
                fw.dma("pool", C.odaT[2 * h:2 * h + 2, :, q0:q0 + nq].rearrange("c p t -> p c t"), sa[:, :, :nq], reads=[sb], owner=sb)

            deferred = []
            pend = qk_exp(items[0]) if items else None
            oms = []
            for ii, (bi, m, kt) in enumerate(items):
                while deferred and deferred[0][0] <= ii:
                    epilogue(*deferred.pop(0)[1])
                q0, nq = BLOCKS[bi]
                k0, nk = TILES[kt]
                ea, eb = pend
                if ii + 1 < len(items):
                    pend = qk_exp(items[ii + 1])
                C.tick(0.9 * nq / 512.0)
                a0, a0b = ps.bank(3 * m)
                a1, a1b = ps.bank(3 * m + 1)
                ad, adb = ps.bank(3 * m + 2)
                first, last = (kt == 0), (kt == len(TILES) - 1)
                fw.op("pe", lambda e: e.matmul(a0[:, :nq], lhsT=vh[:nk, kt, 0:128], rhs=ea[:nk, :nq], start=first, stop=last), reads=[vb_, eb], writes=[a0b])
                fw.op("pe", lambda e: e.matmul(a1[:, :nq], lhsT=vh[:nk, kt, 128:256], rhs=ea[:nk, :nq], start=first, stop=last), reads=[vb_, eb], writes=[a1b])
                fw.op("pe", lambda e: e.matmul(ad[:, :nq], lhsT=C.ones_bf[:nk, :], rhs=ea[:nk, :nq], start=first, stop=last), reads=[C.b_ones, eb], writes=[adb])
                if not last:
                    continue
                rd, rdb = rd_r.next()
                fw.op("act", lambda e: e.copy(out=rd[:, :nq], in_=ad[:, :nq]), reads=[adb], writes=[rdb])
                fw.op("dve", lambda e: e.reciprocal(out=rd[:, :nq], in_=rd[:, :nq]), reads=[rdb], writes=[rdb])
                om, omb = om_r.next()
                fw.op("act", lambda e: e.copy(out=om[:, 0, :nq], in_=a0[:, :nq]), reads=[a0b], writes=[omb])
                fw.op("act", lambda e: e.copy(out=om[:, 1, :nq], in_=a1[:, :nq]), reads=[a1b], pwrites=[omb])
                fw.op("dve", lambda e: e.tensor_tensor(out=om[:, 0, :nq], in0=om[:, 0, :nq], in1=rd[:, :nq], op=ALU.mult), reads=[omb, rdb], writes=[omb])
                fw.op("dve", lambda e: e.tensor_tensor(out=om[:, 1, :nq], in0=om[:, 1, :nq], in1=rd[:, :nq], op=ALU.mult), reads=[omb, rdb], writes=[omb])
                oms.append((om, omb))
                if m == 0:
                    continue
                (om0, om0b), (om1, om1b) = oms
                oms = []
                deferred.append((ii + 6, (om0, om0b, om1, om1b, q0, nq)))
            while deferred:
                epilogue(*deferred.pop(0)[1])
        ps.rot = list(range(8))
        fw.barrier()


def phase_CV(C, l):
    P, fw, ps, nc = C.P, C.fw, C.ps, C.nc
    PAD = 15
    with ExitStack() as es:
        cw_t, cw_b = load_small(C, es, "CV_w", C.convw[l], [128, 8, 31])
        cb_t, cb_b = load_small(C, es, "CV_b", C.convb[l], [128, 8])
        lg_t, lg_b = load_small(C, es, "CV_lg", C.convlg[l], [128, 8])
        lb_t, lb_b = load_small(C, es, "CV_lb", C.convlb[l], [128, 8])
        id_t, id_b = load_small(C, es, "CV_id", C.ident_d, [128, 128])
        z = es.enter_context(nc.sbuf_tensor(UN("CV_z"), [128, 8, T + 2 * PAD], BF16))
        zb = [fw.buf("CV_z") for _ in range(8)]
        Dm = es.enter_context(nc.sbuf_tensor(UN("CV_D"), [128, 8, 31, 128], BF16))
        Db = fw.buf("CV_D")
        with ExitStack() as es2:
            a_r = Ring(fw, es2, "CV_a", [T], BF16, 2)
            g_r = Ring(fw, es2, "CV_g", [T], BF16, 2)
            for c in range(8):
                fw.op("dve", lambda e: e.memset(z[:, c, 0:PAD], 0.0), writes=[zb[c]])
                fw.op("dve", lambda e: e.memset(z[:, c, PAD + T:], 0.0), pwrites=[zb[c]])
                aa, ab = a_r.next()
                ga, gb = g_r.next()
                fw.dma("sp", aa, C.cuA[c], writes=[ab])
                fw.dma("sp", ga, C.cuG[c], writes=[gb])
                fw.op("dve", lambda e: e.tensor_tensor(out=z[:, c, PAD:PAD + T], in0=aa, in1=ga, op=ALU.mult), reads=[ab, gb], pwrites=[zb[c]])
                for j in range(31):
                    fw.op("dve", lambda e: e.tensor_scalar(out=Dm[:, c, j, :], in0=id_t[:], scalar1=cw_t[:, c, j:j + 1], scalar2=None, op0=ALU.mult),
                          reads=[id_b, cw_b], pwrites=[Db])
            fw.barrier()
        xc = es.enter_context(nc.sbuf_tensor(UN("CV_x"), [128, 8, 512], F32))
        xb = fw.buf("CV_x")
        sq_r = Ring(fw, es, "CV_sq", [512], F32, 2)
        mu_r = Ring(fw, es, "CV_mu", [3, 512], F32, 1)
        t_r = Ring(fw, es, "CV_t", [512], F32, 3)
        st_r = Ring(fw, es, "CV_st", [8, 512], BF16, 1)
        ps.rot = [2, 3, 4, 5, 6, 7]
        for bi, (t0, nt) in enumerate(BLOCKS):
            s1, s1b = ps.bank(0)
            s2, s2b = ps.bank(1)
            for c in range(8):
                C.tick(7.0 * nt / 512.0)
                pt, pb = ps.next()
                for j in range(31):
                    fw.op("pe", lambda e: e.matmul(pt[:, :nt], lhsT=Dm[:, c, j, :], rhs=z[:, c, t0 + j:t0 + j + nt], start=(j == 0), stop=(j == 30)), reads=[Db, zb[c]], writes=[pb])
                fw.op("act", lambda e: e.activation(out=xc[:, c, :nt], in_=pt[:, :nt], func=AF.Identity, bias=cb_t[:, c:c + 1]), reads=[pb, cb_b], pwrites=[xb] if c else (), writes=() if c else [xb])
                sq, sqb = sq_r.next()
                fw.op("act", lambda e: e.activation(out=sq[:, :nt], in_=xc[:, c, :nt], func=AF.Square), reads=[xb], writes=[sqb])
                fw.op("pe", lambda e: e.matmul(s1[:, :nt], lhsT=C.ones_f[:], rhs=xc[:, c, :nt], start=(c == 0), stop=(c == 7)), reads=[xb, C.b_ones], writes=[s1b])
                fw.op("pe", lambda e: e.matmul(s2[:, :nt], lhsT=C.ones_f[:], rhs=sq[:, :nt], start=(c == 0), stop=(c == 7)), reads=[sqb, C.b_ones], writes=[s2b])
            mu, mub = mu_r.next()
            fw.op("act", lambda e: e.activation(out=mu[:, 0, :nt], in_=s1[:, :nt], func=AF.Copy, scale=1.0 / 1024), reads=[s1b], writes=[mub])
            fw.op("dve", lambda e: e.tensor_tensor(out=mu[:, 1, :nt], in0=mu[:, 0, :nt], in1=mu[:, 0, :nt], op=ALU.mult), reads=[mub], writes=[mub])
            fw.op("act", lambda e: e.activation(out=mu[:, 2, :nt], in_=s2[:, :nt], func=AF.Copy, scale=1.0 / 1024), reads=[s2b, mub], writes=[mub])
            fw.op("dve", lambda e: e.tensor_tensor(out=mu[:, 2, :nt], in0=mu[:, 2, :nt], in1=mu[:, 1, :nt], op=ALU.subtract), reads=[mub], writes=[mub])
            fw.op("act", lambda e: e.activation(out=mu[:, 2, :nt], in_=mu[:, 2, :nt], func=AF.Sqrt, bias=EPS, scale=1.0), reads=[mub], writes=[mub])
            fw.op("dve", lambda e: e.reciprocal(out=mu[:, 2, :nt], in_=mu[:, 2, :nt]), reads=[mub], writes=[mub])
            sa, sb = st_r.next()
            for c in range(8):
                ta, tb = t_r.next()
                fw.op("dve", lambda e: e.tensor_tensor(out=ta[:, :nt], in0=xc[:, c, :nt], in1=mu[:, 0, :nt], op=ALU.subtract), reads=[xb, mub], writes=[tb])
                fw.op("dve", lambda e: e.tensor_tensor(out=ta[:, :nt], in0=ta[:, :nt], in1=mu[:, 2, :nt], op=ALU.mult), reads=[tb, mub], writes=[tb])
                fw.op("act", lambda e: e.activation(out=sa[:, c, :nt], in_=ta[:, :nt], func=AF.Silu, bias=lb_t[:, c:c + 1], scale=lg_t[:, c:c + 1]), reads=[tb, lg_b, lb_b],
                      pwrites=[sb] if c else (), writes=() if c else [sb])
            fw.dma("pool", C.ocvT[:, :, t0:t0 + nt].rearrange("c p t -> p c t"), sa[:, :, :nt], reads=[sb], owner=sb)
        ps.rot = list(range(8))
        fw.barrier()


def phase_MRG(C, l):
    P, fw, ps, nc = C.P, C.fw, C.ps, C.nc
    with ExitStack() as es:
        bc_t, bc_b = load_small(C, es, "MG_bc", C.bconv[l], [128, KC])
        br = es.enter_context(nc.sbuf_tensor(UN("MG_br"), [128, 24, 512], BF16))
        brb = fw.buf("MG_br")
        mg = es.enter_context(nc.sbuf_tensor(UN("MG_mg"), [128, KC, 512], BF16))
        mgb = fw.buf("MG_mg")
        hbuf = es.enter_context(nc.sbuf_tensor(UN("MG_h"), [128, KC, 512], F32))
        hb = fw.buf("MG_h")
        w_r = Ring(fw, es, "MG_w", [8192], BF16, 2)
        wp_r = Ring(fw, es, "MG_wp", [3, 4096], BF16, 2)
        gt_r = Ring(fw, es, "MG_gt", [3, 512], BF16, 3)
        m_r = Ring(fw, es, "MG_m", [3, 512], BF16, 3)
        ho_r = Ring(fw, es, "MG_ho", [512], F32, 3)
        gview = C.gatesT.rearrange("(i c) p t -> c p i t", i=3)
        for bi, (t0, nt) in enumerate(BLOCKS):
            fw.dma("sp", hbuf[:, :, :nt], C.h_src(l, bi), writes=[hb])
            fw.dma("sp", br[:, 0:8, :nt], C.odaT[:, :, t0:t0 + nt].rearrange("c p t -> p c t"), writes=[brb])
            fw.dma("sp", br[:, 8:16, :nt], C.ocvT[:, :, t0:t0 + nt].rearrange("c p t -> p c t"), owner=brb)
            fw.dma("sp", br[:, 16:24, :nt], C.oglaT[:, :, t0:t0 + nt].rearrange("c p t -> p c t"), owner=brb)
            brb.wr = {brb.semname: brb.dcnt}
            for g in range(4):
                wa, wb = wp_r.next()
                for i, nm in enumerate(("w_da_proj", "w_conv_proj", "w_gla_proj")):
                    if i == 0:
                        fw.dma("sp", wa[:, i, :], C.wgroup_ap(nm, l, g), writes=[wb])
                    else:
                        fw.dma("sp", wa[:, i, :], C.wgroup_ap(nm, l, g), owner=wb)
                wb.wr = {wb.semname: wb.dcnt}
                for ci in range(4):
                    oc = g * 4 + ci
                    ys = []
                    for i in range(3):
                        pt, pb = ps.next()
                        wv = wa[:, i, :].rearrange("p (k c) -> p k c", c=512)
                        for k in range(8):
                            fw.op("pe", lambda e: e.matmul(pt[:, :nt], lhsT=wv[:, k, ci * 128:(ci + 1) * 128], rhs=br[:, 8 * i + k, :nt], start=(k == 0), stop=(k == 7)), reads=[wb, brb], writes=[pb])
                        ys.append((pt, pb))
                    ga, gb = gt_r.next()
                    fw.dma("sp", ga[:, :, :nt], gview[oc, :, :, t0:t0 + nt], writes=[gb])
                    C.tick(6.0 * nt / 512.0)
                    ma, mb = m_r.next()
                    (yd, ydb), (yc, ycb), (yg, ygb) = ys
                    fw.op("act", lambda e: e.copy(out=ma[:, 0, :nt], in_=yd[:, :nt]), reads=[ydb], writes=[mb])
                    fw.op("act", lambda e: e.activation(out=ma[:, 1, :nt], in_=yc[:, :nt], func=AF.Identity, bias=bc_t[:, oc:oc + 1]), reads=[ycb, bc_b], pwrites=[mb])
                    fw.op("act", lambda e: e.copy(out=ma[:, 2, :nt], in_=yg[:, :nt]), reads=[ygb], pwrites=[mb])
                    fw.op("dve", lambda e: e.tensor_tensor(out=ma[:, :, :nt], in0=ma[:, :, :nt], in1=ga[:, :, :nt], op=ALU.mult), reads=[mb, gb], writes=[mb])
                    fw.op("dve", lambda e: e.tensor_tensor(out=ma[:, 0, :nt], in0=ma[:, 0, :nt], in1=ma[:, 1, :nt], op=ALU.add), reads=[mb], writes=[mb])
                    fw.op("dve", lambda e: e.tensor_tensor(out=mg[:, oc, :nt], in0=ma[:, 0, :nt], in1=ma[:, 2, :nt], op=ALU.add), reads=[mb], pwrites=[mgb] if oc else (), writes=() if oc else [mgb])
            for g in range(4):
                wa, wb = w_r.next()
                fw.dma("sp", wa, C.wgroup_ap("w_out", l, g), writes=[wb])
                wv = wa.rearrange("p (k c) -> p k c", c=512)
                for ci in range(4):
                    oc = g * 4 + ci
                    pt, pb = ps.next()
                    for k in range(KC):
                        fw.op("pe", lambda e: e.matmul(pt[:, :nt], lhsT=wv[:, k, ci * 128:(ci + 1) * 128], rhs=mg[:, k, :nt], start=(k == 0), stop=(k == KC - 1)), reads=[wb, mgb], writes=[pb])
                    ha, hab = ho_r.next()
                    fw.op("act", lambda e: e.copy(out=ha[:, :nt], in_=pt[:, :nt]), reads=[pb], writes=[hab])
                    fw.op("dve", lambda e: e.tensor_tensor(out=ha[:, :nt], in0=ha[:, :nt], in1=hbuf[:, oc, :nt], op=ALU.add), reads=[hab, hb], writes=[hab])
                    fw.dma("pool", C.hT[oc, :, t0:t0 + nt], ha[:, :nt], reads=[hab], owner=hab)
                    C.tick(4.0)
        C.pre_barrier("MRG", l)
        fw.barrier()


def phase_MLP(C, l):
    P, fw, ps, nc = C.P, C.fw, C.ps, C.nc
    with ExitStack() as es:
        g_t, g_b = load_small(C, es, "ML_g", C.mlpg[l], [128, KC])
        hbuf = es.enter_context(nc.sbuf_tensor(UN("ML_h"), [128, KC, 512], F32))
        hb = fw.buf("ML_h")
        aT = es.enter_context(nc.sbuf_tensor(UN("ML_a"), [128, KC, 512], BF16))
        ab = fw.buf("ML_a")
        fT = es.enter_context(nc.sbuf_tensor(UN("ML_f"), [128, 64, 512], BF16))
        fb = fw.buf("ML_f")
        sqr = Ring(fw, es, "ML_sq", [512], BF16, 3)
        rstd_r = Ring(fw, es, "ML_rstd", [512], F32, 2)
        w_r = Ring(fw, es, "ML_w", [8192], BF16, 2)
        s_r = Ring(fw, es, "ML_s", [512], F32, 2)
        ho_r = Ring(fw, es, "ML_ho", [512], F32, 3)
        for bi, (t0, nt) in enumerate(BLOCKS):
            fw.dma("sp", hbuf[:, :, :nt], C.hT[:, :, t0:t0 + nt].rearrange("c p t -> p c t"), writes=[hb])
            ra, rb = C.rmsnorm_T(hbuf, hb, nt, g_t, None, (sqr, rstd_r))
            for c in range(KC):
                fw.op("dve", lambda e: e.scalar_tensor_tensor(out=aT[:, c, :nt], in0=hbuf[:, c, :nt], scalar=g_t[:, c:c + 1], in1=ra[:, :nt], op0=ALU.mult, op1=ALU.mult),
                      reads=[hb, g_b, rb], pwrites=[ab] if c else (), writes=() if c else [ab])
            for g in range(16):
                wa, wb = w_r.next()
                fw.dma("sp", wa, C.wgroup_ap("w_mlp_in", l, g), writes=[wb])
                C.tick(14.0 * nt / 512.0)
                wv = wa.rearrange("p (k c) -> p k c", c=512)
                for ci in range(4):
                    fc = g * 4 + ci
                    pt, pb = ps.next()
                    for k in range(KC):
                        fw.op("pe", lambda e: e.matmul(pt[:, :nt], lhsT=wv[:, k, ci * 128:(ci + 1) * 128], rhs=aT[:, k, :nt], start=(k == 0), stop=(k == KC - 1)), reads=[wb, ab], writes=[pb])
                    sa, sb = s_r.next()
                    fw.op("act", lambda e: e.activation(out=sa[:, :nt], in_=pt[:, :nt], func=AF.Relu), reads=[pb], writes=[sb])
                    fw.op("dve", lambda e: e.tensor_tensor(out=fT[:, fc, :nt], in0=sa[:, :nt], in1=sa[:, :nt], op=ALU.mult), reads=[sb],
                          pwrites=[fb] if fc else (), writes=() if fc else [fb])
            for oc in range(16):
                wa, wb = w_r.next()
                fw.dma("sp", wa, C.wgroup_ap("w_mlp_out", l, oc), writes=[wb])
                wv = wa.rearrange("p (k c) -> p k c", c=128)
                pt, pb = ps.next()
                for k in range(64):
                    fw.op("pe", lambda e: e.matmul(pt[:, :nt], lhsT=wv[:, k, :], rhs=fT[:, k, :nt], start=(k == 0), stop=(k == 63)), reads=[wb, fb], writes=[pb])
                ha, hab = ho_r.next()
                fw.op("act", lambda e: e.copy(out=ha[:, :nt], in_=pt[:, :nt]), reads=[pb], writes=[hab])
                fw.op("dve", lambda e: e.tensor_tensor(out=ha[:, :nt], in0=ha[:, :nt], in1=hbuf[:, oc, :nt], op=ALU.add), reads=[hab, hb], writes=[hab])
                fw.dma("pool", C.hT[oc, :, t0:t0 + nt], ha[:, :nt], reads=[hab], owner=hab)
                C.tick(14.0)
        C.pre_barrier("MLP", l)
        fw.barrier()


def phase_FIN(C):
    P, fw, ps, nc = C.P, C.fw, C.ps, C.nc
    with ExitStack() as es:
        g_t, g_b = load_small(C, es, "FN_g", C.fing, [128, KC])
        hbuf = es.enter_context(nc.sbuf_tensor(UN("FN_h"), [128, KC, 512], F32))
        hb = fw.buf("FN_h")
        ob = es.enter_context(nc.sbuf_tensor(UN("FN_o"), [128, KC, 512], F32))
        obb = fw.buf("FN_o")
        sqr = Ring(fw, es, "FN_sq", [512], BF16, 3)
        rstd_r = Ring(fw, es, "FN_rstd", [512], F32, 2)
        for bi, (t0, nt) in enumerate(BLOCKS):
            if bi == 0:
                continue
            fw.dma("sp", hbuf[:, :, :nt], C.hT[:, :, t0:t0 + nt].rearrange("c p t -> p c t"), writes=[hb])
            ra, rb = C.rmsnorm_T(hbuf, hb, nt, g_t, None, (sqr, rstd_r))
            for c in range(KC):
                fw.op("dve", lambda e: e.scalar_tensor_tensor(out=ob[:, c, :nt], in0=hbuf[:, c, :nt], scalar=g_t[:, c:c + 1], in1=ra[:, :nt], op0=ALU.mult, op1=ALU.mult),
                      reads=[hb, g_b, rb], pwrites=[obb] if c else (), writes=() if c else [obb])
            fw.dma("pool", C.outT[:, :, t0 - NMETA:t0 - NMETA + nt].rearrange("c p t -> p c t"), ob[:, :, :nt], reads=[obb], owner=obb)
        fw.barrier()


def phase_GLA(C, l):
    P, fw, ps, nc = C.P, C.fw, C.ps, C.nc
    qs = 128.0 ** -0.5
    with ExitStack() as es:
        gm, gmb = load_small(C, es, "GL_gm", C.gmat_d, [128, 6, 128], BF16)
        ci, cib = load_small(C, es, "GL_ci", C.cind_d, [128, 2], BF16)
        ng, ngb = load_small(C, es, "GL_ng", C.glang[l], [128, 2])
        lr = es.enter_context(nc.sbuf_tensor(UN("GL_lr"), [17, 2, T], F32))
        lrb = fw.buf("GL_lr")
        fw.op("dve", lambda e: e.memset(lr[:, :, :], 1.0), writes=[lrb])
        fw.dma("sp", lr[0:16, 0, :], C.glrT[0:16, :], reads=[lrb], owner=lrb)
        fw.dma("sp", lr[0:16, 1, :], C.glrT[16:32, :], owner=lrb)
        lrb.wr = {lrb.semname: lrb.dcnt}
        gw, gwb_ = load_small(C, es, "GL_gw", C.gwb[l].rearrange("d r c -> r d c"), [17, 2, 512])
        of = es.enter_context(nc.sbuf_tensor(UN("GL_of"), [128, 2, T], F32))
        ofb = fw.buf("GL_of")
        qh = es.enter_context(nc.sbuf_tensor(UN("GL_q"), [128, T], BF16))
        kh = es.enter_context(nc.sbuf_tensor(UN("GL_k"), [128, T], BF16))
        ktok = es.enter_context(nc.sbuf_tensor(UN("GL_kt"), [128, 33, 128], BF16))
        vtok = es.enter_context(nc.sbuf_tensor(UN("GL_vt"), [128, 33, 256], BF16))
        e1all = es.enter_context(nc.sbuf_tensor(UN("GL_e1"), [128, 33, 128], F32))
        graw = es.enter_context(nc.sbuf_tensor(UN("GL_gr"), [128, 33, 128], BF16))
        qhb, khb, ktb, vtb = fw.buf("GLq"), fw.buf("GLk"), fw.buf("GLkt"), fw.buf("GLvt")
        e1b, grb = fw.buf("GLe1"), fw.buf("GLgr")
        S = es.enter_context(nc.sbuf_tensor(UN("GL_S"), [128, 256], F32))
        Sbf = es.enter_context(nc.sbuf_tensor(UN("GL_Sbf"), [128, 256], BF16))
        Sb, Sbfb = fw.buf("GLS"), fw.buf("GLSbf")
        f_r = Ring(fw, es, "GL_f", [128], F32, 9)
        b_r = Ring(fw, es, "GL_b", [128], BF16, 12)
        E_r = Ring(fw, es, "GL_E", [2], F32, 4)
        sq_r = Ring(fw, es, "GL_sq", [512], F32, 2)
        rs_r = Ring(fw, es, "GL_rs", [512], F32, 2)
        tm_r = Ring(fw, es, "GL_tm", [512], F32, 2)
        sr_r = Ring(fw, es, "GL_sr", [2, 512], BF16, 2)
        st_r = Ring(fw, es, "GL_st", [2, 512], BF16, 2)
        for h in range(4):
            if h not in P.cfg.get("GLA_heads", range(4)):
                continue
            fw.dma("sp", qh[:, :], C.gqT[h], writes=[qhb])
            fw.dma("sp", kh[:, :], C.gkT[h], writes=[khb])
            fw.dma("sp", ktok[:NMETA, 0, :], C.gktok[0:NMETA, h * 128:(h + 1) * 128], writes=[ktb])
            fw.dma("sp", vtok[:NMETA, 0, :], C.gvtok[0:NMETA, h * 256:(h + 1) * 256], writes=[vtb])
            for i in range(4):
                fw.dma("sp", ktok[:, 1 + 8 * i:9 + 8 * i, :], C.gktok[NMETA + 1024 * i:NMETA + 1024 * (i + 1), h * 128:(h + 1) * 128].rearrange("(n p) c -> p n c", p=128), owner=ktb)
                fw.dma("sp", vtok[:, 1 + 8 * i:9 + 8 * i, :], C.gvtok[NMETA + 1024 * i:NMETA + 1024 * (i + 1), h * 256:(h + 1) * 256].rearrange("(n p) c -> p n c", p=128), owner=vtb)
            ktb.wr = {ktb.semname: ktb.dcnt}
            vtb.wr = {vtb.semname: vtb.dcnt}
            for d in range(2):
                px, pxb = ps.next()
                fw.op("pe", lambda e: e.matmul(px[:NMETA, :128], lhsT=lr[:, d, 0:NMETA], rhs=gw[:, d, h * 128:(h + 1) * 128], start=True, stop=True), reads=[lrb, gwb_], writes=[pxb])
                fw.op("act", lambda e: e.activation(out=e1all[:NMETA, 0, :], in_=px[:NMETA, :128], func=AF.Exp, scale=-1.0), reads=[pxb], writes=[e1b])
                for g in range(8):
                    px, pxb = ps.next()
                    for j in range(4):
                        t0 = NMETA + 128 * (4 * g + j)
                        fw.op("pe", lambda e: e.matmul(px[:, j * 128:(j + 1) * 128], lhsT=lr[:, d, t0:t0 + 128], rhs=gw[:, d, h * 128:(h + 1) * 128], start=True, stop=True), reads=[lrb, gwb_], writes=[pxb])
                    fw.op("act", lambda e: e.activation(out=e1all[:, 1 + 4 * g:5 + 4 * g, :], in_=px[:, :].rearrange("p (a b) -> p a b", b=128), func=AF.Exp, scale=-1.0), reads=[pxb], pwrites=[e1b])
                fw.op("act", lambda e: e.activation(out=graw[:NMETA, 0, :], in_=e1all[:NMETA, 0, :], func=AF.Ln, bias=1.0), reads=[e1b], writes=[grb])
                fw.op("act", lambda e: e.activation(out=graw[:, 1:33, :], in_=e1all[:, 1:33, :], func=AF.Ln, bias=1.0), reads=[e1b], pwrites=[grb])
                fw.op("dve", lambda e: e.memset(S[:, :], 0.0), writes=[Sb])
                fw.op("dve", lambda e: e.memset(Sbf[:, :], 0.0), writes=[Sbfb])
                order = list(enumerate(TILES))
                if d == 1:
                    order = order[::-1]

                def pre(ti, t0, n):
                    gr = graw[:, ti, :]
                    pb_, pbb = ps.next()
                    fw.op("pe", lambda e: e.matmul(pb_[:, :n], lhsT=gr[:n, :], rhs=gm[:n, d, :n], start=True, stop=True), reads=[grb, gmb], writes=[pbb])
                    pr, prb = ps.next()
                    fw.op("pe", lambda e: e.matmul(pr[:n, :128], lhsT=gm[:n, 2 + d, :n], rhs=gr[:n, :], start=True, stop=True), reads=[grb, gmb], writes=[prb])
                    pc, pcb = ps.next()
                    fw.op("pe", lambda e: e.matmul(pc[:, 0:2], lhsT=gr[:n, :], rhs=ci[:n, 0:2], start=True, stop=True), reads=[grb, cib], writes=[pcb])
                    eb, ebb = f_r.next()
                    fw.op("act", lambda e: e.activation(out=eb[:, :n], in_=pb_[:, :n], func=AF.Exp), reads=[pbb], writes=[ebb])
                    enb, enbb = f_r.next()
                    fw.op("act", lambda e: e.activation(out=enb[:, :n], in_=pb_[:, :n], func=AF.Exp, scale=-1.0), reads=[pbb], writes=[enbb])
                    er, erb = f_r.next()
                    fw.op("act", lambda e: e.activation(out=er[:n, :], in_=pr[:n, :128], func=AF.Exp), reads=[prb], writes=[erb])
                    E, Eb = E_r.next()
                    fw.op("act", lambda e: e.activation(out=E[:, 0:2], in_=pc[:, 0:2], func=AF.Exp), reads=[pcb], writes=[Eb])
                    qt, qtb = b_r.next()
                    fw.op("dve", lambda e: e.scalar_tensor_tensor(out=qt[:, :n], in0=qh[:, t0:t0 + n], scalar=qs, in1=eb[:, :n], op0=ALU.mult, op1=ALU.mult), reads=[qhb, ebb], writes=[qtb])
                    kt, ktb2 = b_r.next()
                    fw.op("dve", lambda e: e.tensor_tensor(out=kt[:, :n], in0=kh[:, t0:t0 + n], in1=enb[:, :n], op=ALU.mult), reads=[khb, enbb], writes=[ktb2])
                    kha, khab = b_r.next()
                    fw.op("dve", lambda e: e.tensor_tensor(out=kha[:n, :], in0=ktok[:n, ti, :], in1=er[:n, :], op=ALU.mult), reads=[ktb, erb], writes=[khab])
                    pa, pab = ps.next()
                    fw.op("pe", lambda e: e.matmul(pa[:n, :n], lhsT=kt[:, :n], rhs=qt[:, :n], start=True, stop=True), reads=[ktb2, qtb], writes=[pab])
                    atm, atmb = b_r.next()
                    fw.op("dve", lambda e: e.tensor_tensor(out=atm[:n, :n], in0=pa[:n, :n], in1=gm[:n, 4 + d, :n], op=ALU.mult), reads=[pab, gmb], writes=[atmb])
                    return (qt, qtb, kha, khab, atm, atmb, E, Eb)

                def scan(ti, t0, n, hd):
                    qt, qtb, kha, khab, atm, atmb, E, Eb = hd
                    os_ = [ps.next(), ps.next()]
                    for vc in range(2):
                        o, ob_ = os_[vc]
                        fw.op("pe", lambda e: e.matmul(o[:, :n], lhsT=vtok[:n, ti, vc * 128:(vc + 1) * 128], rhs=atm[:n, :n], start=True, stop=False), reads=[vtb, atmb], writes=[ob_])
                    chunks = [(0, n, 0)] if n <= 64 else [(0, 64, 0), (64, 128, 1)]
                    if d == 1:
                        chunks = chunks[::-1]
                    for cidx_i, (c0, c1, cx) in enumerate(chunks):
                        lastc = cidx_i == len(chunks) - 1
                        for vc in range(2):
                            o, ob_ = os_[vc]
                            fw.op("pe", lambda e: e.matmul(o[:, c0:c1], lhsT=Sbf[:, vc * 128:(vc + 1) * 128], rhs=qt[:, c0:c1], start=False, stop=lastc), reads=[Sbfb, qtb], writes=[ob_])
                        pS, pSb = ps.next()
                        fw.op("pe", lambda e: e.matmul(pS[:, :256], lhsT=kha[c0:c1, :], rhs=vtok[c0:c1, ti, :], start=True, stop=True), reads=[khab, vtb], writes=[pSb])
                        fw.op("dve", lambda e: e.scalar_tensor_tensor(out=Sbf[:, :], in0=S[:, :], scalar=E[:, cx:cx + 1], in1=pS[:, :256], op0=ALU.mult, op1=ALU.add), reads=[Sb, Eb, pSb], writes=[Sbfb])
                        fw.op("dve", lambda e: e.scalar_tensor_tensor(out=S[:, :], in0=S[:, :], scalar=E[:, cx:cx + 1], in1=pS[:, :256], op0=ALU.mult, op1=ALU.add), reads=[Sb, Eb, pSb], writes=[Sb])
                    for vc in range(2):
                        o, ob_ = os_[vc]
                        if d == 0:
                            fw.op("act", lambda e: e.copy(out=of[:, vc, t0:t0 + n], in_=o[:, :n]), reads=[ob_], pwrites=[ofb])
                        else:
                            fw.op("dve", lambda e: e.tensor_tensor(out=of[:, vc, t0:t0 + n], in0=o[:, :n], in1=of[:, vc, t0:t0 + n], op=ALU.add), reads=[ob_, ofb], pwrites=[ofb])

                hd = pre(order[0][0], *order[0][1])
                for oi, (ti, (t0, n)) in enumerate(order):
                    hn = pre(order[oi + 1][0], *order[oi + 1][1]) if oi + 1 < len(order) else None
                    scan(ti, t0, n, hd)
                    hd = hn
                    C.tick(5.0)
            for bi, (t0, nt) in enumerate(BLOCKS):
                pn, pnb = ps.next()
                for vc in range(2):
                    sq, sqb = sq_r.next()
                    fw.op("act", lambda e: e.activation(out=sq[:, :nt], in_=of[:, vc, t0:t0 + nt], func=AF.Square), reads=[ofb], writes=[sqb])
                    fw.op("pe", lambda e: e.matmul(pn[:, :nt], lhsT=C.ones_f[:], rhs=sq[:, :nt], start=(vc == 0), stop=(vc == 1)), reads=[sqb, C.b_ones], writes=[pnb])
                rs, rsb = rs_r.next()
                fw.op("act", lambda e: e.activation(out=rs[:, :nt], in_=pn[:, :nt], func=AF.Sqrt, bias=EPS, scale=1.0 / 256), reads=[pnb], writes=[rsb])
                fw.op("dve", lambda e: e.reciprocal(out=rs[:, :nt], in_=rs[:, :nt]), reads=[rsb], writes=[rsb])
                sr, srb = sr_r.next()
                fw.dma("sp", sr[:, :, :nt], C.srT[2 * h:2 * h + 2, :, t0:t0 + nt].rearrange("c p t -> p c t"), writes=[srb])
                sa, sb = st_r.next()
                for vc in range(2):
                    tm, tmb = tm_r.next()
                    fw.op("dve", lambda e: e.scalar_tensor_tensor(out=tm[:, :nt], in0=of[:, vc, t0:t0 + nt], scalar=ng[:, vc:vc + 1], in1=rs[:, :nt], op0=ALU.mult, op1=ALU.mult), reads=[ofb, ngb, rsb], writes=[tmb])
                    fw.op("dve", lambda e: e.tensor_tensor(out=sa[:, vc, :nt], in0=tm[:, :nt], in1=sr[:, vc, :nt], op=ALU.mult), reads=[tmb, srb], pwrites=[sb] if vc else (), writes=() if vc else [sb])
                fw.dma("pool", C.oglaT[2 * h:2 * h + 2, :, t0:t0 + nt].rearrange("c p t -> p c t"), sa[:, :, :nt], reads=[sb], owner=sb)
        C.pre_barrier("GLA", l)
        fw.barrier()
```
